# Optimizing a Trainium2 kernel written in Bass

```python
import math
import jax, jax.numpy as jnp
from jax import lax
import numpy as np

D_MODEL = 1024
BATCH = 2
SEQ = 8192
DEPTH = 2

ATT_HEADS = 16
ATT_KV_HEADS = 2
ATT_HEAD_DIM = 64
ATT_WIDTH = ATT_HEADS * ATT_HEAD_DIM
KV_WIDTH = ATT_KV_HEADS * ATT_HEAD_DIM
WINDOW = 128
ATT_BLOCK = 128
REL_BUCKETS = 32
REL_MAX_DIST = 128
SG_GROUPS = 8
SG_CHUNK = 128
SG_WIDTH = 1024
SG_GROUP_DIM = SG_WIDTH // SG_GROUPS
SSM_WIDTH = 2 * D_MODEL
SSM_HEAD_DIM = 64
SSM_HEADS = SSM_WIDTH // SSM_HEAD_DIM
SSM_GROUPS = 4
SSM_STATE = 128
SSM_CONV = 4
SSM_CHUNK = 128
SSM_CONV_DIM = SSM_WIDTH + 2 * SSM_GROUPS * SSM_STATE
N_BRANCHES = 3
IN_SIZES = (ATT_WIDTH, KV_WIDTH, KV_WIDTH, ATT_WIDTH,
            SG_WIDTH, SG_WIDTH, SG_WIDTH,
            SSM_WIDTH, SSM_CONV_DIM, SSM_HEADS,
            N_BRANCHES * D_MODEL)
IN_COLS = sum(IN_SIZES)
EPS = 1e-6

kernel_name = "hybrid_swa_sgu_ssd_gated_merge"


def _split_points():
    return [int(v) for v in np.cumsum(np.array(IN_SIZES))[:-1]]


def rms_norm(x, g):
    xf = x.astype(jnp.float32)
    y = xf * lax.rsqrt(jnp.mean(xf * xf, axis=-1, keepdims=True) + EPS)
    return (y * g.astype(jnp.float32)).astype(x.dtype)


def t5_causal_bucket(dist):
    max_exact = REL_BUCKETS // 2
    dist_f = jnp.maximum(dist, 1).astype(jnp.float32)
    large = max_exact + (jnp.log(dist_f / max_exact) / math.log(REL_MAX_DIST / max_exact)
                         * (REL_BUCKETS - max_exact)).astype(jnp.int32)
    large = jnp.minimum(large, REL_BUCKETS - 1)
    return jnp.where(dist < max_exact, dist, large)


def sliding_window_attention(q, k, v, sinks, rel_bias):
    bsz, seq = q.shape[:2]
    nb = seq // ATT_BLOCK
    grp = ATT_HEADS // ATT_KV_HEADS
    qb = q.reshape(bsz, nb, ATT_BLOCK, ATT_KV_HEADS, grp, ATT_HEAD_DIM) * (ATT_HEAD_DIM ** -0.5)

    def band(t):
        tb = t.reshape(bsz, nb, ATT_BLOCK, ATT_KV_HEADS, ATT_HEAD_DIM)
        prev = jnp.pad(tb, ((0, 0), (1, 0), (0, 0), (0, 0), (0, 0)))[:, :-1]
        return jnp.concatenate([prev, tb], axis=2)

    kk, vv = band(k), band(v)
    logits = jnp.einsum('bnqkgd,bnskd->bnkgqs', qb, kk).astype(jnp.float32)

    qi = jnp.arange(ATT_BLOCK, dtype=jnp.int32)[:, None]
    kj = jnp.arange(2 * ATT_BLOCK, dtype=jnp.int32)[None, :]
    dist = qi + ATT_BLOCK - kj
    in_window = (dist >= 0) & (dist < WINDOW)
    key_exists = (jnp.arange(nb)[:, None] > 0) | (kj >= ATT_BLOCK)
    mask = in_window[None] & key_exists[:, None, :]

    bias = rel_bias.astype(jnp.float32)[t5_causal_bucket(jnp.maximum(dist, 0))]
    bias = jnp.transpose(bias, (2, 0, 1)).reshape(ATT_KV_HEADS, grp, ATT_BLOCK, 2 * ATT_BLOCK)
    logits = jnp.where(mask[None, :, None, None], logits + bias[None, None], -jnp.inf)

    sink = sinks.astype(jnp.float32).reshape(ATT_KV_HEADS, grp)[None, None, :, :, None, None]
    m = jnp.maximum(jnp.max(logits, axis=-1, keepdims=True), sink)
    p = jnp.exp(logits - m)
    p = p / (jnp.sum(p, axis=-1, keepdims=True) + jnp.exp(sink - m))
    out = jnp.einsum('bnkgqs,bnskd->bnqkgd', p.astype(vv.dtype), vv)
    return out.reshape(bsz, seq, ATT_WIDTH)


def chunked_spatial_gate(u, v, ln_g, ln_b, w_s, b_s):
    bsz, seq = u.shape[:2]
    nc = seq // SG_CHUNK
    vf = v.astype(jnp.float32)
    mu = jnp.mean(vf, axis=-1, keepdims=True)
    var = jnp.mean(jnp.square(vf - mu), axis=-1, keepdims=True)
    vn = ((vf - mu) * lax.rsqrt(var + EPS) * ln_g.astype(jnp.float32) + ln_b.astype(jnp.float32)).astype(v.dtype)
    vc = vn.reshape(bsz, nc, SG_CHUNK, SG_GROUPS, SG_GROUP_DIM)
    causal = jnp.tril(jnp.ones((SG_CHUNK, SG_CHUNK), dtype=bool))
    w = jnp.where(causal[None], w_s, jnp.zeros_like(w_s))
    mixed = jnp.einsum('gts,bcsgd->bctgd', w, vc) + jnp.transpose(b_s)[None, None, :, :, None]
    return u * mixed.reshape(bsz, seq, SG_WIDTH)


def causal_depthwise_conv(x, w, b):
    ch = x.shape[-1]
    y = lax.conv_general_dilated(x, w[:, None, :].astype(x.dtype), window_strides=(1,),
                                 padding=[(SSM_CONV - 1, 0)],
                                 dimension_numbers=('NWC', 'WIO', 'NWC'),
                                 feature_group_count=ch)
    return y + b


def ssd_mixer(z, xbc, dt_raw, conv_w, conv_b, dt_bias, a_log, d_skip, norm_g):
    bsz, seq = z.shape[:2]
    nc = seq // SSM_CHUNK
    hpg = SSM_HEADS // SSM_GROUPS
    L = SSM_CHUNK
    xbc = jax.nn.silu(causal_depthwise_conv(xbc, conv_w, conv_b))
    gn = SSM_GROUPS * SSM_STATE
    xs = xbc[..., :SSM_WIDTH]
    b_in = xbc[..., SSM_WIDTH:SSM_WIDTH + gn]
    c_in = xbc[..., SSM_WIDTH + gn:]

    dt = jax.nn.softplus(dt_raw.astype(jnp.float32) + dt_bias.astype(jnp.float32))
    a = -jnp.exp(a_log.astype(jnp.float32))
    x_heads = xs.astype(jnp.float32).reshape(bsz, seq, SSM_HEADS, SSM_HEAD_DIM)
    xdt = (x_heads * dt[..., None]).reshape(bsz, nc, L, SSM_GROUPS, hpg, SSM_HEAD_DIM)
    bc = b_in.astype(jnp.float32).reshape(bsz, nc, L, SSM_GROUPS, SSM_STATE)
    cc = c_in.astype(jnp.float32).reshape(bsz, nc, L, SSM_GROUPS, SSM_STATE)

    a_dt = (dt * a).reshape(bsz, nc, L, SSM_GROUPS, hpg).transpose(0, 3, 4, 1, 2)
    a_cs = jnp.cumsum(a_dt, axis=-1)

    causal = jnp.tril(jnp.ones((L, L), dtype=bool))
    seg = a_cs[..., :, None] - a_cs[..., None, :]
    decay_in = jnp.exp(jnp.where(causal, seg, -jnp.inf))
    cb = jnp.einsum('bclgn,bcsgn->bcgls', cc, bc)
    y_diag = jnp.einsum('bcgls,bgjcls,bcsgjp->bclgjp', cb, decay_in, xdt)

    decay_to_end = jnp.exp(a_cs[..., -1:] - a_cs)
    states = jnp.einsum('bcsgn,bgjcs,bcsgjp->bcgjpn', bc, decay_to_end, xdt)
    chunk_decay = jnp.exp(a_cs[..., -1])

    def carry_state(h, inp):
        dec, st = inp
        return h * dec[..., None, None] + st, h

    init = jnp.zeros((bsz, SSM_GROUPS, hpg, SSM_HEAD_DIM, SSM_STATE), jnp.float32)
    _, prev = lax.scan(carry_state, init, (jnp.moveaxis(chunk_decay, -1, 0), jnp.moveaxis(states, 1, 0)))
    prev = jnp.moveaxis(prev, 0, 1)
    y_off = jnp.einsum('bclgn,bcgjpn,bgjcl->bclgjp', cc, prev, jnp.exp(a_cs))

    y = (y_diag + y_off).reshape(bsz, seq, SSM_HEADS, SSM_HEAD_DIM) + d_skip.astype(jnp.float32)[:, None] * x_heads
    y = y.reshape(bsz, seq, SSM_WIDTH) * jax.nn.silu(z.astype(jnp.float32))
    yg = y.reshape(bsz, seq, SSM_GROUPS, SSM_WIDTH // SSM_GROUPS)
    yg = yg * lax.rsqrt(jnp.mean(yg * yg, axis=-1, keepdims=True) + EPS)
    y = yg.reshape(bsz, seq, SSM_WIDTH) * norm_g.astype(jnp.float32)
    return y.astype(z.dtype)


def hybrid_layer(x, w_in, g_pre, g_post, rel_bias, sinks, sg_ln_g, sg_ln_b, sg_w, sg_b,
                 conv_w, conv_b, dt_bias, a_log, d_skip, ssm_norm_g,
                 w_br_att, w_br_sg, w_br_ssm, w_out):
    bsz, seq = x.shape[:2]
    h = rms_norm(x, g_pre)
    proj = jnp.einsum('bsd,dc->bsc', h, w_in)
    (q, k, v, z_a, u, v_s, z_s, z_m, xbc, dt_raw, gate_logits) = jnp.split(proj, _split_points(), axis=-1)

    q = q.reshape(bsz, seq, ATT_HEADS, ATT_HEAD_DIM)
    k = k.reshape(bsz, seq, ATT_KV_HEADS, ATT_HEAD_DIM)
    v = v.reshape(bsz, seq, ATT_KV_HEADS, ATT_HEAD_DIM)
    y_att = sliding_window_attention(q, k, v, sinks, rel_bias) * jax.nn.silu(z_a)
    y_sg = chunked_spatial_gate(u, v_s, sg_ln_g, sg_ln_b, sg_w, sg_b) * jax.nn.silu(z_s)
    y_ssm = ssd_mixer(z_m, xbc, dt_raw, conv_w, conv_b, dt_bias, a_log, d_skip, ssm_norm_g)

    gates = jax.nn.sigmoid(gate_logits.reshape(bsz, seq, N_BRANCHES, D_MODEL))
    merged = (gates[:, :, 0] * (y_att @ w_br_att)
              + gates[:, :, 1] * (y_sg @ w_br_sg)
              + gates[:, :, 2] * (y_ssm @ w_br_ssm))
    out = merged @ w_out
    return x + rms_norm(out, g_post)


def setup_inputs(seed: int = 0) -> dict:
    key = jax.random.key(seed)
    ks = jax.random.split(key, 24)
    f32 = jnp.float32
    nrm = lambda k, shape, s: (jax.random.normal(k, shape, f32) * s)
    dt0 = jnp.exp(jax.random.uniform(ks[12], (DEPTH, SSM_HEADS), f32, math.log(1e-3), math.log(1e-1)))
    return {
        "x": nrm(ks[0], (BATCH, SEQ, D_MODEL), 1.0),
        "w_in": nrm(ks[1], (DEPTH, D_MODEL, IN_COLS), D_MODEL ** -0.5),
        "norm_pre": 1.0 + nrm(ks[2], (DEPTH, D_MODEL), 0.02),
        "norm_post": 1.0 + nrm(ks[3], (DEPTH, D_MODEL), 0.02),
        "rel_bias": nrm(ks[4], (REL_BUCKETS, ATT_HEADS), 0.5),
        "att_sinks": nrm(ks[5], (DEPTH, ATT_HEADS), 0.5),
        "sg_ln_g": 1.0 + nrm(ks[6], (DEPTH, SG_WIDTH), 0.02),
        "sg_ln_b": nrm(ks[7], (DEPTH, SG_WIDTH), 0.02),
        "sg_w": nrm(ks[8], (DEPTH, SG_GROUPS, SG_CHUNK, SG_CHUNK), SG_CHUNK ** -0.5),
        "sg_b": 1.0 + nrm(ks[9], (DEPTH, SG_GROUPS, SG_CHUNK), 0.02),
        "ssm_conv_w": nrm(ks[10], (DEPTH, SSM_CONV, SSM_CONV_DIM), SSM_CONV ** -0.5),
        "ssm_conv_b": nrm(ks[11], (DEPTH, SSM_CONV_DIM), 0.02),
        "ssm_dt_bias": dt0 + jnp.log(-jnp.expm1(-dt0)),
        "ssm_a_log": jnp.log(jax.random.uniform(ks[13], (DEPTH, SSM_HEADS), f32, 1.0, 16.0)),
        "ssm_d": 1.0 + nrm(ks[14], (DEPTH, SSM_HEADS), 0.02),
        "ssm_norm_g": 1.0 + nrm(ks[15], (DEPTH, SSM_WIDTH), 0.02),
        "w_br_att": nrm(ks[16], (DEPTH, ATT_WIDTH, D_MODEL), ATT_WIDTH ** -0.5),
        "w_br_sg": nrm(ks[17], (DEPTH, SG_WIDTH, D_MODEL), SG_WIDTH ** -0.5),
        "w_br_ssm": nrm(ks[18], (DEPTH, SSM_WIDTH, D_MODEL), SSM_WIDTH ** -0.5),
        "w_out": nrm(ks[19], (DEPTH, D_MODEL, D_MODEL), D_MODEL ** -0.5),
    }


def reference(x, w_in, norm_pre, norm_post, rel_bias, att_sinks, sg_ln_g, sg_ln_b, sg_w, sg_b,
              ssm_conv_w, ssm_conv_b, ssm_dt_bias, ssm_a_log, ssm_d, ssm_norm_g,
              w_br_att, w_br_sg, w_br_ssm, w_out):
    for layer in range(DEPTH):
        x = hybrid_layer(x, w_in[layer], norm_pre[layer], norm_post[layer], rel_bias,
                         att_sinks[layer], sg_ln_g[layer], sg_ln_b[layer], sg_w[layer], sg_b[layer],
                         ssm_conv_w[layer], ssm_conv_b[layer], ssm_dt_bias[layer], ssm_a_log[layer],
                         ssm_d[layer], ssm_norm_g[layer],
                         w_br_att[layer], w_br_sg[layer], w_br_ssm[layer], w_out[layer])
    return x
```

```python
from contextlib import ExitStack
from concourse.bass_utils import run_bass_kernel_spmd
import numpy as np
import concourse.bass as bass
import concourse.mybir as mybir

F32 = mybir.dt.float32
BF16 = mybir.dt.bfloat16
AF = mybir.ActivationFunctionType
ALU = mybir.AluOpType
AX = mybir.AxisListType


class Region:
    __slots__ = ("name", "last_w", "reads", "dsem")

    def __init__(self, name, dsem=None):
        self.name = name
        self.last_w = None
        self.reads = []
        self.dsem = dsem


class DmaSem:
    def __init__(self, name):
        self.name = name
        self.sem = None
        self.count = 0


class Instr:
    __slots__ = ("eng", "fn", "r", "w", "deps", "is_dma", "dsem", "dma_idx",
                 "milestone", "ms_idx", "dma_before")

    def __init__(self, eng, fn, r, w, is_dma=False, dsem=None):
        self.eng = eng
        self.fn = fn
        self.r = r
        self.w = w
        self.deps = set()
        self.is_dma = is_dma
        self.dsem = dsem
        self.dma_idx = 0
        self.milestone = False
        self.ms_idx = 0
        self.dma_before = {}


ENGS = ("pe", "act", "dve", "pool", "sp")


class Prog:
    def __init__(self, nc, same_engine_sync=True):
        self.nc = nc
        self.instrs = []
        self.same_engine_sync = same_engine_sync
        self.eng_obj = {"pe": nc.tensor, "act": nc.scalar, "dve": nc.vector,
                        "pool": nc.gpsimd, "sp": nc.sync}
        self.dsems = []
        self.finals_l = []
        self._dsem_count_now = {}

    def dsem(self, name):
        d = DmaSem(name)
        self.dsems.append(d)
        return d

    def region(self, name, dsem=None):
        return Region(name, dsem)

    def I(self, eng, fn, r=(), w=()):
        ins = Instr(eng, fn, list(r), list(w))
        self._add(ins)
        return ins

    def dma(self, eng, fn, r=(), w=(), dsem=None):
        if dsem is None:
            for reg in list(w) + list(r):
                if reg.dsem is not None:
                    dsem = reg.dsem
                    break
        assert dsem is not None, "dma needs a DmaSem"
        ins = Instr(eng, fn, list(r), list(w), is_dma=True, dsem=dsem)
        self._add(ins)
        dsem.count += 1
        ins.dma_idx = dsem.count
        return ins

    def _add(self, ins):
        idx = len(self.instrs)
        for reg in ins.r:
            if reg.last_w is not None:
                ins.deps.add(reg.last_w)
        for reg in ins.w:
            if reg.last_w is not None:
                ins.deps.add(reg.last_w)
            for rd in reg.reads:
                ins.deps.add(rd)
        ins.deps.discard(idx)
        for d in ins.deps:
            di = self.instrs[d]
            if di.is_dma:
                ins.dma_before[di.dsem] = di.dsem.count
        for reg in ins.w:
            reg.last_w = idx
            reg.reads = []
        for reg in ins.r:
            if reg.last_w != idx:
                reg.reads.append(idx)
        self.instrs.append(ins)

    def emit(self):
        nc = self.nc
        instrs = self.instrs
        for i, ins in enumerate(instrs):
            for d in ins.deps:
                di = instrs[d]
                if di.is_dma:
                    continue
                if di.eng == ins.eng and (ins.eng == "pe" or not self.same_engine_sync):
                    continue
                di.milestone = True
        cnt = {e: 0 for e in ENGS}
        for ins in instrs:
            if ins.milestone:
                cnt[ins.eng] += 1
                ins.ms_idx = cnt[ins.eng]
        import contextlib
        stack = contextlib.ExitStack()
        sems = {}
        for e in ENGS:
            sems[e] = stack.enter_context(nc.semaphore("s_" + e))
        for d in self.dsems:
            if d.count > 0:
                d.sem = stack.enter_context(nc.semaphore("d_" + d.name))
        seen = {e: {} for e in ENGS}
        nwait = 0
        plan = {e: [] for e in ENGS}
        for ins in instrs:
            need = {}
            for d in ins.deps:
                di = instrs[d]
                if di.is_dma:
                    key = ("d", di.dsem)
                    val = 16 * ins.dma_before[di.dsem]
                    sem = di.dsem.sem
                else:
                    if di.eng == ins.eng and (ins.eng == "pe" or not self.same_engine_sync):
                        continue
                    key = ("e", di.eng)
                    val = di.ms_idx
                    sem = sems[di.eng]
                if need.get(key, (None, 0))[1] < val:
                    need[key] = (sem, val)
            waits = []
            for key, (sem, val) in need.items():
                if seen[ins.eng].get(key, 0) < val:
                    waits.append((sem, val))
                    seen[ins.eng][key] = val
                    nwait += 1
            plan[ins.eng].append((ins, waits))
        finals = {}
        for en, d in self.finals_l:
            finals.setdefault(en, []).append(d)

        def run_stream(ename, eo):
            for ins, waits in plan[ename]:
                for sem, val in waits:
                    eo.wait_ge(sem, val)
                res = ins.fn(eo)
                if ins.is_dma:
                    res.then_inc(ins.dsem.sem, 16)
                elif ins.milestone:
                    res.then_inc(sems[ins.eng], 1)
            for d in finals.get(ename, []):
                eo.wait_ge(d.sem, 16 * d.count)

        with nc.Block() as block:
            @block.tensor
            def _(e):
                run_stream("pe", e)

            @block.scalar
            def _(e):
                run_stream("act", e)

            @block.vector
            def _(e):
                run_stream("dve", e)

            @block.gpsimd
            def _(e):
                run_stream("pool", e)

            @block.sync
            def _(e):
                run_stream("sp", e)
        self.sems = sems
        self.stats = dict(n=len(instrs), waits=nwait, ms=dict(cnt))
        stack.close()

    def final_wait(self, eng, dsem):
        self.finals_l.append((eng, dsem))


D = 1024
NV = 6256
GPRE, GPOST, LNG, LNB, NG, SINK, DTB, ALOG, DSK = 0, 1024, 2048, 3072, 4096, 6144, 6160, 6192, 6224
EPS = 1e-6
NEG = -30000.0
NSLOT = 3

TILES = []
TILES += [("in", 0, 0, 512), ("in", 0, 512, 512)]
TILES += [("kd", 0, 0, 256)]
TILES += [("in", 0, 1152, 128)]
TILES += [("in", 0, 1280 + 512 * i, 512) for i in range(2)]
TILES += [("in", 0, 2304 + 512 * i, 512) for i in range(6)]
TILES += [("in", 0, 5376 + 512 * i, 512) for i in range(4)]
TILES += [("in", 0, 7424 + 512 * i, 512) for i in range(6)]
TILES += [("in", 0, 10496, 32)]
TILES += [("in", 0, 10528 + 512 * i, 512) for i in range(6)]
TILES += [("ba", 0, 0, 512), ("ba", 0, 512, 512)]
TILES += [("bs", 0, 0, 512), ("bs", 0, 512, 512)]
TILES += [("bm", 0, 0, 512), ("bm", 0, 512, 512), ("bm", 1024, 0, 512), ("bm", 1024, 512, 512)]
TILES += [("o", 0, 0, 512), ("o", 0, 512, 512)]
NTILE = len(TILES)


def build(S, depth=2):
    NCH = S // 128
    nc = bass.Bass("TRN2", target_bir_lowering=False)
    dt_in = lambda name, shape, dt=F32: nc.dram_tensor(name, shape, dt, kind="ExternalInput").ap()
    x_d = dt_in("x", [S, D])
    wsrc = {"in": dt_in("w_in", [2, D, 13600]), "kd": dt_in("w_kd", [2, D, 256]),
            "ba": dt_in("w_ba", [2, D, D]), "bs": dt_in("w_bs", [2, D, D]),
            "bm": dt_in("w_bm", [2, 2048, D]), "o": dt_in("w_o", [2, D, D])}
    vecs_d = dt_in("vecs", [2, 128, NV])
    convw_d = dt_in("convw", [2, 128, 24, 5])
    sgb_d = dt_in("sgb", [2, 128, 8])
    sgwT_d = dt_in("sgwT", [2, 128, 8, 128])
    biasT_d = dt_in("biasT", [128, 4, 2, 4, 128])
    consts_d = dt_in("consts", [128, 512])
    y_d = nc.dram_tensor("y", [S, D], F32, kind="ExternalOutput").ap()
    wbf = nc.dram_tensor("wbf", [2, NTILE, 128, 8, 512], BF16, kind="Internal").ap()

    P = Prog(nc)
    es = ExitStack()

    class T:
        def __init__(self, name, shape, dt, psum=False, dma=False):
            if psum:
                self.t = es.enter_context(nc.psum_tensor("p_" + name, shape, dt))
            else:
                self.t = es.enter_context(nc.sbuf_tensor("s_" + name, shape, dt))
            self.r = P.region(name, P.dsem(name) if dma else None)

    cst = T("cst", [128, 512], F32, dma=True)
    identf = cst.t[:, 0:128]; U = cst.t[:, 128:256]; Lm = cst.t[:, 256:384]; ones = cst.t[:, 384:512]
    identb = T("identb", [128, 128], BF16)
    biasT = T("biasT", [128, 4, 2, 4, 128], F32, dma=True)
    vec = [T("vec%d" % l, [128, 6144], BF16, dma=True) for l in range(2)]
    vsm = [T("vsm%d" % l, [128, 112], F32, dma=True) for l in range(2)]
    esink = [T("esink%d" % l, [128, 16], F32) for l in range(2)]
    avec = [T("avec%d" % l, [128, 32], F32) for l in range(2)]
    cw = [T("cw%d" % l, [128, 24, 5], F32, dma=True) for l in range(2)]
    sgb = [T("sgb%d" % l, [128, 8], F32, dma=True) for l in range(2)]
    sgst = T("sgst", [128, 8, 128], F32, dma=True)
    WmT = [T("WmT%d" % l, [128, 8, 128], BF16) for l in range(2)]
    kdp = [T("kdp%d" % l, [128, 2, 128], BF16) for l in range(2)]
    vsb = [T("vsb%d" % l, [128, 2, 2, 65], BF16) for l in range(2)]
    halo = [T("halo%d" % l, [128, 24, 3], F32) for l in range(2)]
    St = [T("S%d" % l, [128, 2048], F32) for l in range(2)]
    Sb = [T("Sb%d" % l, [128, 2048], BF16) for l in range(2)]
    wt = [T("wt%d" % i, [128, 8, 512], BF16, dma=True) for i in range(NSLOT)]
    xa = [T("xa%d" % i, [128, D], F32, dma=True) for i in range(2)]
    x1 = T("x1", [128, D], F32)
    xo = T("xo", [128, D], F32, dma=True)
    sq = T("sq", [128, D], F32)
    ss = T("ss", [128, 16], F32)
    hb = T("hb", [128, D], BF16)
    hT = T("hT", [128, 8, 128], BF16)
    qT = T("qT", [128, 8, 128], BF16)
    kdc = T("kdc", [128, 2, 128], BF16)
    za = T("za", [128, D], BF16)
    scs0 = T("scs", [128, 4, 128], F32); scs = [scs0, scs0]
    PT = T("PT", [128, 2, 4, 128], BF16)
    den = T("den", [128, 8], F32)
    ytm = T("ytm", [128, 2048], BF16)
    yT = T("yT", [128, 32, 128], BF16)
    usb = T("usb", [128, D], F32)
    vss = T("vss", [128, D], F32)
    zs = za
    vn = T("vn", [128, D], BF16)
    zm = T("zm", [128, 2048], BF16)
    stg = T("stg", [128, 131], F32)
    cacc = T("cacc", [128, 128], F32)
    xcT = T("xcT", [128, 24, 128], BF16)
    xtm = T("xtm", [128, 2048], BF16)
    Btm = T("Btm", [128, 4, 128], BF16)
    sm = T("sm", [128, 8, 32], F32)
    AL = T("AL", [128, 8, 128], F32)
    CBm = T("CBm", [128, 4, 128], F32)
    dec4 = T("dec4", [128, 4, 128], F32)
    MT = T("MT", [128, 4, 128], BF16)
    xdt = T("xdt", [128, 2048], BF16)
    xw = T("xw", [128, 2048], BF16)
    yssm = T("yssm", [128, 2048], F32)
    gate = T("gate", [128, 3072], BF16)
    osb = usb; merged = vss; mb = hb; mT = hT; xD = ytm

    class View:
        def __init__(self, base, ap):
            self.t = ap; self.r = base.r
    tmpf = View(sq, sq.t[:, 0:256]); ytmp = View(sq, sq.t[:, 0:512]); mtmp = View(sq, sq.t[:, 512:1024])
    accs = [T("acc%d" % i, [128, 512], F32, psum=True) for i in range(2)]
    pT = T("pT", [128, 8, 128], BF16, psum=True)
    pS = [T("pS%d" % i, [128, 4, 128], F32, psum=True) for i in range(2)]
    pV = T("pV", [128, 512], F32, psum=True)
    pY = [T("pY%d" % i, [128, 512], F32, psum=True) for i in range(2)]

    wbfsem = [P.dsem("wbf%d" % l) for l in range(2)]
    Rwbf = [[P.region("wbf%d_%d" % (l, t), wbfsem[l]) for t in range(NTILE)] for l in range(2)]
    ysem = P.dsem("y")
    Ry = P.region("ydram", ysem)

    I = P.I
    ACT_COPY = AF.Copy

    def acopy(out, in_, r, w, scale=None):
        if scale is None:
            I("act", lambda e: e.activation(out=out, in_=in_, func=ACT_COPY), r=r, w=w)
        else:
            I("act", lambda e: e.activation(out=out, in_=in_, func=ACT_COPY, scale=scale), r=r, w=w)

    def act(out, in_, func, r, w, **kw):
        I("act", lambda e: e.activation(out=out, in_=in_, func=func, **kw), r=r, w=w)

    def tt(out, in0, in1, op, r, w, eng="dve"):
        I(eng, lambda e: e.tensor_tensor(out=out, in0=in0, in1=in1, op=op), r=r, w=w)

    def ts(out, in0, s1, s2, op0, op1, r, w, eng="dve"):
        if op1 is None:
            I(eng, lambda e: e.tensor_scalar(out=out, in0=in0, scalar1=s1, scalar2=None, op0=op0), r=r, w=w)
        else:
            I(eng, lambda e: e.tensor_scalar(out=out, in0=in0, scalar1=s1, scalar2=s2, op0=op0, op1=op1), r=r, w=w)

    def stt(out, in0, scalar, in1, op0, op1, r, w):
        I("dve", lambda e: e.scalar_tensor_tensor(out=out, in0=in0, scalar=scalar, in1=in1, op0=op0, op1=op1), r=r, w=w)

    def mm(out, lhsT, rhs, start, stop, r, w):
        I("pe", lambda e: e.matmul(out, lhsT=lhsT, rhs=rhs, start=start, stop=stop), r=r, w=w)

    def tp(out, in_, r, w):
        I("pe", lambda e: e.transpose(out=out, in_=in_, identity=identb.t[:]), r=r + [identb.r], w=w)

    for l in range(depth):
        for i3 in range(3):
            P.dma("pool", lambda e, l=l, i3=i3: e.dma_start(out=vec[l].t[:, i3 * 2048:(i3 + 1) * 2048],
                                                          in_=vecs_d[l, :, i3 * 2048:(i3 + 1) * 2048]), w=[vec[l].r])
    wtsw = [P.dsem("wtsw%d" % i) for i in range(NSLOT)]
    for l in range(depth):
        for t, (src, row0, col0, n) in enumerate(TILES):
            sap = wsrc[src][l, row0:row0 + 1024, col0:col0 + n].rearrange("(k p) n -> p k n", p=128)
            slot = (l * NTILE + t) % NSLOT
            P.dma("pool", lambda e, slot=slot, n=n, sap=sap: e.dma_start(out=wt[slot].t[:, :, 0:n], in_=sap),
                  w=[wt[slot].r], dsem=wtsw[slot])
            P.dma("sp", lambda e, l=l, t=t, n=n, slot=slot: e.dma_start(out=wbf[l, t, :, :, 0:n], in_=wt[slot].t[:, :, 0:n]),
                  r=[wt[slot].r], w=[Rwbf[l][t]], dsem=Rwbf[l][t].dsem)

    P.dma("sp", lambda e: e.dma_start(out=cst.t[:], in_=consts_d[:, :]), w=[cst.r])
    P.dma("sp", lambda e: e.dma_start(out=biasT.t[:], in_=biasT_d[:, :, :, :, :]), w=[biasT.r])
    I("dve", lambda e: e.tensor_copy(out=identb.t[:], in_=identf), r=[cst.r], w=[identb.r])
    for l in range(depth):
        P.dma("sp", lambda e, l=l: e.dma_start(out=vsm[l].t[:], in_=vecs_d[l, :, 6144:6256]), w=[vsm[l].r])
        P.dma("sp", lambda e, l=l: e.dma_start(out=cw[l].t[:], in_=convw_d[l, :, :, :]), w=[cw[l].r])
        P.dma("sp", lambda e, l=l: e.dma_start(out=sgb[l].t[:], in_=sgb_d[l, :, :]), w=[sgb[l].r])
        P.dma("sp", lambda e, l=l: e.dma_start(out=sgst.t[:], in_=sgwT_d[l, :, :, :]), w=[sgst.r])
        tt(WmT[l].t[:], sgst.t[:], cst.t[:, 128:256].unsqueeze(1).broadcast_to([128, 8, 128]), ALU.mult,
           r=[sgst.r, cst.r], w=[WmT[l].r])
        act(esink[l].t[:], vsm[l].t[:, 0:16], AF.Exp, r=[vsm[l].r], w=[esink[l].r])
        act(avec[l].t[:], vsm[l].t[:, 48:80], AF.Exp, r=[vsm[l].r], w=[avec[l].r])
        ts(avec[l].t[:], avec[l].t[:], -1.0, None, ALU.mult, None, r=[avec[l].r], w=[avec[l].r])
        I("dve", lambda e, l=l: e.memset(kdp[l].t[:], 0.0), w=[kdp[l].r])
        I("dve", lambda e, l=l: e.memset(vsb[l].t[:], 0.0), w=[vsb[l].r])
        I("dve", lambda e, l=l: e.memset(vsb[l].t[:, :, :, 64:65], 1.0), w=[vsb[l].r])
        I("dve", lambda e, l=l: e.memset(halo[l].t[:], 0.0), w=[halo[l].r])
        I("dve", lambda e, l=l: e.memset(St[l].t[:], 0.0), w=[St[l].r])
        I("dve", lambda e, l=l: e.memset(Sb[l].t[:], 0.0), w=[Sb[l].r])

    state = {"slot": 0, "acc": 0}

    def load_w(l, t):
        slot = state["slot"]; state["slot"] = (slot + 1) % NSLOT
        n = TILES[t][3]
        P.dma("sp", lambda e: e.dma_start(out=wt[slot].t[:, :, 0:n], in_=wbf[l, t, :, :, 0:n]),
              r=[Rwbf[l][t]], w=[wt[slot].r])
        return slot

    def next_acc():
        a = accs[state["acc"]]; state["acc"] ^= 1
        return a

    def proj_tm(l, t, consume):
        slot = load_w(l, t); n = TILES[t][3]; a = next_acc()
        for k in range(8):
            mm(a.t[:, 0:n], hT.t[:, k, :], wt[slot].t[:, k, 0:n], k == 0, k == 7, r=[hT.r, wt[slot].r], w=[a.r])
        consume(a, n)

    def proj_fm(l, t, consume):
        slot = load_w(l, t); n = TILES[t][3]
        for j in range(n // 128):
            a = next_acc()
            for k in range(8):
                mm(a.t[:, 0:128], wt[slot].t[:, k, j * 128:(j + 1) * 128], hT.t[:, k, :], k == 0, k == 7,
                   r=[hT.r, wt[slot].r], w=[a.r])
            consume(a, j)

    def transposes(src, nt, dst_ap_fn, r, w):
        for r0 in range(0, nt, 8):
            m = min(8, nt - r0)
            for j in range(m):
                tp(pT.t[:, j, :], src(r0 + j), r=r, w=[pT.r])
            acopy(dst_ap_fn(r0, m), pT.t[:, 0:m, :], r=[pT.r], w=w)

    def rstd_from(col, scale):
        c = ss.t[:, col:col + 1]
        ts(c, c, scale, EPS, ALU.mult, ALU.add, r=[ss.r], w=[ss.r])
        act(c, c, AF.Sqrt, r=[ss.r], w=[ss.r])
        I("dve", lambda e: e.reciprocal(out=c, in_=c), r=[ss.r], w=[ss.r])

    def chunk_layer(l, c, xin, xout):
        V = vec[l]
        act(sq.t[:], xin.t[:], AF.Square, r=[xin.r], w=[sq.r, ss.r], accum_out=ss.t[:, 0:1])
        rstd_from(0, 1.0 / D)
        stt(hb.t[:], xin.t[:], ss.t[:, 0:1], V.t[:, GPRE:GPRE + D], ALU.mult, ALU.mult, r=[xin.r, ss.r, V.r], w=[hb.r])
        transposes(lambda j: hb.t[:, j * 128:(j + 1) * 128], 8, lambda r0, m: hT.t[:, r0:r0 + m, :], r=[hb.r], w=[hT.r])

        for ti in range(2):
            proj_fm(l, ti, lambda a, j, ti=ti: acopy(qT.t[:, ti * 4 + j, :], a.t[:, 0:128], r=[a.r], w=[qT.r], scale=0.125))
        proj_fm(l, 2, lambda a, j: acopy(kdc.t[:, j, :], a.t[:, 0:128], r=[a.r], w=[kdc.r]))
        proj_tm(l, 3, lambda a, n: acopy(vsb[l].t[:, 1, :, 0:64], a.t[:, 0:128].rearrange("p (g d) -> p g d", g=2), r=[a.r], w=[vsb[l].r]))
        for i in range(2):
            proj_tm(l, 4 + i, lambda a, n, i=i: act(za.t[:, i * 512:(i + 1) * 512], a.t[:, :], AF.Silu, r=[a.r], w=[za.r]))
        kbs = [1] if c == 0 else [0, 1]
        sl = slice(2, 4) if c == 0 else slice(0, 4)
        for hq in range(4):
            g = hq // 2
            for rg in range(2):
                r0 = rg * 64
                for kb in kbs:
                    keys = kdp[l] if kb == 0 else kdc
                    for i in range(2):
                        j = 2 * hq + i
                        mm(pS[rg].t[:, kb * 2 + i, :], keys.t[r0:r0 + 64, g, :], qT.t[r0:r0 + 64, j, :], True, True,
                           r=[keys.r, qT.r], w=[pS[rg].r])
                tt(scs[rg].t[:, sl, :], pS[rg].t[:, sl, :], biasT.t[:, hq, rg, sl, :], ALU.add, r=[pS[rg].r, biasT.r], w=[scs[rg].r])
                act(PT.t[:, rg, sl, :], scs[rg].t[:, sl, :], AF.Exp, r=[scs[rg].r], w=[PT.r])
            pVv = pV.t[:, 0:260].rearrange("p (h e) -> p h e", e=65)
            for hh in range(4):
                rg = hh % 2; i = hh // 2
                for kb in kbs:
                    mm(pVv[:, hh, :], PT.t[:, rg, kb * 2 + i, :], vsb[l].t[:, kb, g, :], kb == kbs[0], kb == 1,
                       r=[PT.r, vsb[l].r], w=[pV.r])
            dv = den.t[:, 0:4].unsqueeze(2)
            tt(dv, pVv[:, :, 64:65], esink[l].t[:, 4 * hq:4 * hq + 4].unsqueeze(2), ALU.add, r=[pV.r, esink[l].r], w=[den.r])
            I("dve", lambda e: e.reciprocal(out=den.t[:, 0:4], in_=den.t[:, 0:4]), r=[den.r], w=[den.r])
            tt(tmpf.t[:].rearrange("p (h d) -> p h d", d=64), pVv[:, :, 0:64], dv.broadcast_to([128, 4, 64]), ALU.mult,
               r=[pV.r, den.r], w=[tmpf.r])
            tt(ytm.t[:, hq * 256:(hq + 1) * 256], tmpf.t[:], za.t[:, hq * 256:(hq + 1) * 256], ALU.mult, r=[tmpf.r, za.r], w=[ytm.r])
        transposes(lambda j: ytm.t[:, j * 128:(j + 1) * 128], 8, lambda r0, m: yT.t[:, r0:r0 + m, :], r=[ytm.r], w=[yT.r])
        acopy(kdp[l].t[:], kdc.t[:], r=[kdc.r], w=[kdp[l].r])
        acopy(vsb[l].t[:, 0, :, 0:64], vsb[l].t[:, 1, :, 0:64], r=[vsb[l].r], w=[vsb[l].r])

        for i in range(2):
            proj_tm(l, 6 + i, lambda a, n, i=i: acopy(usb.t[:, i * 512:(i + 1) * 512], a.t[:, :], r=[a.r], w=[usb.r]))
        for i in range(2):
            proj_tm(l, 8 + i, lambda a, n, i=i: acopy(vss.t[:, i * 512:(i + 1) * 512], a.t[:, :], r=[a.r], w=[vss.r]))
        for i in range(2):
            proj_tm(l, 10 + i, lambda a, n, i=i: act(zs.t[:, i * 512:(i + 1) * 512], a.t[:, :], AF.Silu, r=[a.r], w=[zs.r]))
        act(sq.t[:], vss.t[:], AF.Copy, r=[vss.r], w=[sq.r, ss.r], accum_out=ss.t[:, 1:2])
        act(sq.t[:], vss.t[:], AF.Square, r=[vss.r], w=[sq.r, ss.r], accum_out=ss.t[:, 2:3])
        ts(ss.t[:, 3:4], ss.t[:, 1:2], 1.0 / D, None, ALU.mult, None, r=[ss.r], w=[ss.r])
        tt(ss.t[:, 4:5], ss.t[:, 3:4], ss.t[:, 3:4], ALU.mult, r=[ss.r], w=[ss.r])
        ts(ss.t[:, 5:6], ss.t[:, 2:3], 1.0 / D, ss.t[:, 4:5], ALU.mult, ALU.subtract, r=[ss.r], w=[ss.r])
        rstd_from(5, 1.0)
        ts(ss.t[:, 6:7], ss.t[:, 3:4], ss.t[:, 5:6], -1.0, ALU.mult, ALU.mult, r=[ss.r], w=[ss.r])
        act(sq.t[:], vss.t[:], AF.Identity, r=[vss.r, ss.r], w=[sq.r], scale=ss.t[:, 5:6], bias=ss.t[:, 6:7])
        tt(sq.t[:], sq.t[:], V.t[:, LNG:LNG + D], ALU.mult, r=[sq.r, V.r], w=[sq.r])
        tt(vn.t[:], sq.t[:], V.t[:, LNB:LNB + D], ALU.add, r=[sq.r, V.r], w=[vn.r])
        for g in range(8):
            mm(pY[g // 4].t[:, (g % 4) * 128:(g % 4 + 1) * 128], WmT[l].t[:, g, :], vn.t[:, g * 128:(g + 1) * 128], True, True,
               r=[WmT[l].r, vn.r], w=[pY[g // 4].r])
        for hf in range(2):
            cs = slice(hf * 512, (hf + 1) * 512)
            tt(sq.t[:, cs].rearrange("p (g d) -> p g d", d=128), pY[hf].t[:].rearrange("p (g d) -> p g d", d=128),
               sgb[l].t[:, 4 * hf:4 * hf + 4].unsqueeze(2).broadcast_to([128, 4, 128]), ALU.add, r=[pY[hf].r, sgb[l].r], w=[sq.r])
            tt(sq.t[:, cs], sq.t[:, cs], usb.t[:, cs], ALU.mult, r=[sq.r, usb.r], w=[sq.r])
            tt(ytm.t[:, cs], sq.t[:, cs], zs.t[:, cs], ALU.mult, r=[sq.r, zs.r], w=[ytm.r])
        transposes(lambda j: ytm.t[:, j * 128:(j + 1) * 128], 8, lambda r0, m: yT.t[:, 8 + r0:8 + r0 + m, :], r=[ytm.r], w=[yT.r])

        for i in range(4):
            proj_tm(l, 12 + i, lambda a, n, i=i: act(zm.t[:, i * 512:(i + 1) * 512], a.t[:, :], AF.Silu, r=[a.r], w=[zm.r]))

        def conv_consume(a, j, ti):
            jj = ti * 4 + j
            acopy(stg.t[:, 3:131], a.t[:, 0:128], r=[a.r], w=[stg.r])
            acopy(stg.t[:, 0:3], halo[l].t[:, jj, :], r=[halo[l].r], w=[stg.r])
            ts(cacc.t[:], stg.t[:, 0:128], cw[l].t[:, jj, 0:1], cw[l].t[:, jj, 4:5], ALU.mult, ALU.add, r=[stg.r, cw[l].r], w=[cacc.r])
            for k in range(1, 4):
                stt(cacc.t[:], stg.t[:, k:k + 128], cw[l].t[:, jj, k:k + 1], cacc.t[:], ALU.mult, ALU.add, r=[stg.r, cw[l].r, cacc.r], w=[cacc.r])
            act(xcT.t[:, jj, :], cacc.t[:], AF.Silu, r=[cacc.r], w=[xcT.r])
            acopy(halo[l].t[:, jj, :], stg.t[:, 128:131], r=[stg.r], w=[halo[l].r])
        for ti in range(6):
            proj_fm(l, 16 + ti, lambda a, j, ti=ti: conv_consume(a, j, ti))

        def dt_consume(a, n):
            tt(sm.t[:, 0, :], a.t[:, 0:32], vsm[l].t[:, 16:48], ALU.add, r=[a.r, vsm[l].r], w=[sm.r])
            act(sm.t[:, 0, :], sm.t[:, 0, :], AF.Exp, r=[sm.r], w=[sm.r])
            act(sm.t[:, 0, :], sm.t[:, 0, :], AF.Ln, r=[sm.r], w=[sm.r], bias=1.0)
            tt(sm.t[:, 1, :], sm.t[:, 0, :], avec[l].t[:], ALU.mult, r=[sm.r, avec[l].r], w=[sm.r])
        proj_tm(l, 22, dt_consume)
        transposes(lambda j: xcT.t[:, j, :], 16, lambda r0, m: xtm.t[:, r0 * 128:(r0 + m) * 128].rearrange("p (j d) -> p j d", d=128),
                   r=[xcT.r], w=[xtm.r])
        transposes(lambda j: xcT.t[:, 16 + j, :], 4, lambda r0, m: Btm.t[:, 0:4, :], r=[xcT.r], w=[Btm.r])
        mm(pV.t[:, 0:32], U, sm.t[:, 1, :], True, True, r=[cst.r, sm.r], w=[pV.r])
        mm(pV.t[:, 32:64], ones, sm.t[:, 1, :], True, True, r=[cst.r, sm.r], w=[pV.r])
        acopy(sm.t[:, 2, :], pV.t[:, 0:32], r=[pV.r], w=[sm.r])
        acopy(sm.t[:, 4, :], pV.t[:, 32:64], r=[pV.r], w=[sm.r])
        act(sm.t[:, 3, :], sm.t[:, 2, :], AF.Exp, r=[sm.r], w=[sm.r])
        tt(sm.t[:, 5, :], sm.t[:, 4, :], sm.t[:, 2, :], ALU.subtract, r=[sm.r], w=[sm.r])
        act(sm.t[:, 5, :], sm.t[:, 5, :], AF.Exp, r=[sm.r], w=[sm.r])
        act(sm.t[:, 6, :], sm.t[:, 4, :], AF.Exp, r=[sm.r], w=[sm.r])
        v3 = lambda t_, : t_.t[:].rearrange("p (h d) -> p h d", d=64)
        bc = lambda ap: ap.unsqueeze(2).broadcast_to([128, 32, 64])
        tt(v3(xdt), v3(xtm), bc(sm.t[:, 0, :]), ALU.mult, r=[xtm.r, sm.r], w=[xdt.r])
        tt(v3(xD), v3(xtm), bc(vsm[l].t[:, 80:112]), ALU.mult, r=[xtm.r, vsm[l].r], w=[xD.r])
        tt(v3(xw), v3(xdt), bc(sm.t[:, 5, :]), ALU.mult, r=[xdt.r, sm.r], w=[xw.r])
        for g in range(4):
            mm(pS[0].t[:, g, :], xcT.t[:, 16 + g, :], xcT.t[:, 20 + g, :], True, True, r=[xcT.r], w=[pS[0].r])
        tt(CBm.t[:], pS[0].t[:], U.unsqueeze(1).broadcast_to([128, 4, 128]), ALU.mult, r=[pS[0].r, cst.r], w=[CBm.r])
        for g in range(4):
            gs = slice(g * 512, (g + 1) * 512)
            tt(AL.t[:], Lm.unsqueeze(1).broadcast_to([128, 8, 128]), sm.t[:, 1, 8 * g:8 * g + 8].unsqueeze(2).broadcast_to([128, 8, 128]), ALU.mult,
               r=[cst.r, sm.r], w=[AL.r])
            mm(pY[0].t[:], identb.t[:], xD.t[:, gs], True, False, r=[identb.r, xD.r], w=[pY[0].r])
            if c > 0:
                mm(pY[1].t[:], xcT.t[:, 20 + g, :], Sb[l].t[:, gs], True, True, r=[xcT.r, Sb[l].r], w=[pY[1].r])
            for hf in range(2):
                for hh in range(4):
                    h = 8 * g + 4 * hf + hh
                    mm(pS[1].t[:, hh, :], AL.t[:, 4 * hf + hh, :], U, True, True, r=[AL.r, cst.r], w=[pS[1].r])
                act(dec4.t[:], pS[1].t[:], AF.Exp, r=[pS[1].r], w=[dec4.r])
                tt(MT.t[:], dec4.t[:], CBm.t[:, g, :].unsqueeze(1).broadcast_to([128, 4, 128]), ALU.mult, r=[dec4.r, CBm.r], w=[MT.r])
                for hh in range(4):
                    h = 8 * g + 4 * hf + hh
                    mm(pY[0].t[:, (4 * hf + hh) * 64:(4 * hf + hh + 1) * 64], MT.t[:, hh, :], xdt.t[:, h * 64:(h + 1) * 64],
                       False, (hf == 1 and hh == 3), r=[MT.r, xdt.r], w=[pY[0].r])
            if c > 0:
                tt(ytmp.t[:].rearrange("p (h d) -> p h d", d=64), pY[1].t[:].rearrange("p (h d) -> p h d", d=64),
                   sm.t[:, 3, 8 * g:8 * g + 8].unsqueeze(2).broadcast_to([128, 8, 64]), ALU.mult, r=[pY[1].r, sm.r], w=[ytmp.r])
                tt(yssm.t[:, gs], pY[0].t[:], ytmp.t[:], ALU.add, r=[pY[0].r, ytmp.r], w=[yssm.r])
            else:
                acopy(yssm.t[:, gs], pY[0].t[:], r=[pY[0].r], w=[yssm.r])
        for g in range(4):
            gs = slice(g * 512, (g + 1) * 512)
            a = next_acc()
            mm(a.t[:], Btm.t[:, g, :], xw.t[:, gs], True, True, r=[Btm.r, xw.r], w=[a.r])
            if c > 0:
                tt(St[l].t[:, gs].rearrange("p (h d) -> p h d", d=64), St[l].t[:, gs].rearrange("p (h d) -> p h d", d=64),
                   sm.t[:, 6, 8 * g:8 * g + 8].unsqueeze(2).broadcast_to([128, 8, 64]), ALU.mult, r=[St[l].r, sm.r], w=[St[l].r])
                tt(St[l].t[:, gs], St[l].t[:, gs], a.t[:], ALU.add, r=[St[l].r, a.r], w=[St[l].r])
            else:
                acopy(St[l].t[:, gs], a.t[:], r=[a.r], w=[St[l].r])
        acopy(Sb[l].t[:], St[l].t[:], r=[St[l].r], w=[Sb[l].r])
        tt(yssm.t[:], yssm.t[:], zm.t[:], ALU.mult, r=[yssm.r, zm.r], w=[yssm.r])
        for g in range(4):
            act(sq.t[:, 0:512], yssm.t[:, g * 512:(g + 1) * 512], AF.Square, r=[yssm.r], w=[sq.r, ss.r], accum_out=ss.t[:, 8 + g:9 + g])
        ts(ss.t[:, 8:12], ss.t[:, 8:12], 1.0 / 512, EPS, ALU.mult, ALU.add, r=[ss.r], w=[ss.r])
        act(ss.t[:, 8:12], ss.t[:, 8:12], AF.Sqrt, r=[ss.r], w=[ss.r])
        I("dve", lambda e: e.reciprocal(out=ss.t[:, 8:12], in_=ss.t[:, 8:12]), r=[ss.r], w=[ss.r])
        tt(yssm.t[:].rearrange("p (g d) -> p g d", d=512), yssm.t[:].rearrange("p (g d) -> p g d", d=512),
           ss.t[:, 8:12].unsqueeze(2).broadcast_to([128, 4, 512]), ALU.mult, r=[yssm.r, ss.r], w=[yssm.r])
        tt(ytm.t[:], yssm.t[:], V.t[:, NG:NG + 2048], ALU.mult, r=[yssm.r, V.r], w=[ytm.r])
        transposes(lambda j: ytm.t[:, j * 128:(j + 1) * 128], 16, lambda r0, m: yT.t[:, 16 + r0:16 + r0 + m, :], r=[ytm.r], w=[yT.r])

        for i in range(6):
            proj_tm(l, 23 + i, lambda a, n, i=i: act(gate.t[:, i * 512:(i + 1) * 512], a.t[:, :], AF.Sigmoid, r=[a.r], w=[gate.r]))

        for br, (t0, kb0, nk) in enumerate([(29, 0, 1), (31, 8, 1), (33, 16, 2)]):
            for ct in range(2):
                cs = slice(ct * 512, (ct + 1) * 512)
                slots = [load_w(l, t0 + ct + 2 * kh) for kh in range(nk)]
                a = next_acc()
                for kh in range(nk):
                    for k in range(8):
                        mm(a.t[:], yT.t[:, kb0 + kh * 8 + k, :], wt[slots[kh]].t[:, k, :], (kh == 0 and k == 0), (kh == nk - 1 and k == 7),
                           r=[yT.r, wt[slots[kh]].r], w=[a.r])
                gsl = gate.t[:, br * 1024 + ct * 512: br * 1024 + (ct + 1) * 512]
                if br == 0:
                    tt(merged.t[:, cs], a.t[:], gsl, ALU.mult, r=[a.r, gate.r], w=[merged.r])
                else:
                    tt(mtmp.t[:], a.t[:], gsl, ALU.mult, r=[a.r, gate.r], w=[mtmp.r])
                    tt(merged.t[:, cs], merged.t[:, cs], mtmp.t[:], ALU.add, r=[merged.r, mtmp.r], w=[merged.r])
        acopy(mb.t[:], merged.t[:], r=[merged.r], w=[mb.r])
        transposes(lambda j: mb.t[:, j * 128:(j + 1) * 128], 8, lambda r0, m: mT.t[:, r0:r0 + m, :], r=[mb.r], w=[mT.r])
        for ct in range(2):
            slot = load_w(l, 37 + ct)
            a = next_acc()
            for k in range(8):
                mm(a.t[:], mT.t[:, k, :], wt[slot].t[:, k, :], k == 0, k == 7, r=[mT.r, wt[slot].r], w=[a.r])
            acopy(osb.t[:, ct * 512:(ct + 1) * 512], a.t[:], r=[a.r], w=[osb.r])
        act(sq.t[:], osb.t[:], AF.Square, r=[osb.r], w=[sq.r, ss.r], accum_out=ss.t[:, 12:13])
        rstd_from(12, 1.0 / D)
        stt(osb.t[:], osb.t[:], ss.t[:, 12:13], V.t[:, GPOST:GPOST + D], ALU.mult, ALU.mult, r=[osb.r, ss.r, V.r], w=[osb.r])
        tt(xout.t[:], xin.t[:], osb.t[:], ALU.add, r=[xin.r, osb.r], w=[xout.r])

    def load_x(c):
        b = xa[c % 2]
        P.dma("sp", lambda e: e.dma_start(out=b.t[:], in_=x_d[c * 128:(c + 1) * 128, :]), w=[b.r])

    load_x(0)
    for c in range(NCH):
        if c + 1 < NCH:
            load_x(c + 1)
        if depth == 2:
            chunk_layer(0, c, xa[c % 2], x1)
            chunk_layer(1, c, x1, xo)
        else:
            chunk_layer(0, c, xa[c % 2], xo)
        P.dma("sp", lambda e, c=c: e.dma_start(out=y_d[c * 128:(c + 1) * 128, :], in_=xo.t[:]), r=[xo.r], w=[Ry], dsem=ysem)
    P.final_wait("sp", ysem)
    P.emit()
    es.close()
    return nc, P


def _t5_bucket(dist):
    max_exact = 16
    dist_f = np.maximum(dist, 1).astype(np.float32)
    large = max_exact + (np.log(dist_f / np.float32(max_exact)) / np.float32(np.log(128 / max_exact))
                         * np.float32(32 - max_exact)).astype(np.int32)
    large = np.minimum(large, 31)
    return np.where(dist < max_exact, dist, large)


def host_prep(inputs, S):
    f = lambda a: np.ascontiguousarray(np.asarray(a, dtype=np.float32))
    w_in = f(inputs["w_in"])
    kcols = w_in[:, :, 1024:1152]
    w_kd = np.concatenate([kcols[:, :, 0:64], kcols[:, :, 0:64], kcols[:, :, 64:128], kcols[:, :, 64:128]], axis=2)
    vec = np.concatenate([f(inputs["norm_pre"]), f(inputs["norm_post"]), f(inputs["sg_ln_g"]), f(inputs["sg_ln_b"]),
                          f(inputs["ssm_norm_g"]), f(inputs["att_sinks"]), f(inputs["ssm_dt_bias"]), f(inputs["ssm_a_log"]),
                          f(inputs["ssm_d"])], axis=1)
    vecs = np.ascontiguousarray(np.broadcast_to(vec[:, None, :], (2, 128, NV)))
    cwf = np.concatenate([f(inputs["ssm_conv_w"]), f(inputs["ssm_conv_b"])[:, None, :]], axis=1)
    convw = np.ascontiguousarray(cwf.reshape(2, 5, 24, 128).transpose(0, 3, 2, 1))
    sgb = np.ascontiguousarray(f(inputs["sg_b"]).transpose(0, 2, 1))
    sgwT = np.ascontiguousarray(f(inputs["sg_w"]).transpose(0, 3, 1, 2))
    rel = f(inputs["rel_bias"])
    table = np.concatenate([rel, np.full((1, 16), NEG, np.float32)], axis=0)
    s = np.arange(128)[:, None]; q = np.arange(128)[None, :]
    d0 = q + 128 - s; valid0 = (d0 >= 0) & (d0 < 128)
    d1 = q - s; valid1 = d1 >= 0
    idx0 = np.where(valid0, _t5_bucket(np.maximum(d0, 0)), 32)
    idx1 = np.where(valid1, _t5_bucket(np.maximum(d1, 0)), 32)
    idx = np.stack([idx0, idx1], axis=1)
    bt = table[idx].transpose(0, 1, 3, 2)
    bt = bt.reshape(128, 2, 4, 2, 2, 128)
    biasT = np.ascontiguousarray(bt.transpose(0, 2, 4, 1, 3, 5).reshape(128, 4, 2, 4, 128))
    k = np.arange(128)[:, None]; j = np.arange(128)[None, :]
    consts = np.concatenate([(k == j), (k <= j), (k > j), np.ones((128, 128), bool)], axis=1).astype(np.float32)
    common = {"w_in": w_in, "w_kd": np.ascontiguousarray(w_kd), "w_ba": f(inputs["w_br_att"]), "w_bs": f(inputs["w_br_sg"]),
              "w_bm": f(inputs["w_br_ssm"]), "w_o": f(inputs["w_out"]), "vecs": vecs, "convw": convw, "sgb": sgb,
              "sgwT": sgwT, "biasT": biasT, "consts": np.ascontiguousarray(consts)}
    return common


_CACHE = {}


def kernel(**inputs):
    x = np.asarray(inputs["x"], dtype=np.float32)
    B, S, _ = x.shape
    if S not in _CACHE:
        _CACHE[S] = build(S)[0]
    nc = _CACHE[S]
    common = host_prep(inputs, S)
    in_maps = []
    for core in range(8):
        m = dict(common)
        m["x"] = np.ascontiguousarray(x[core % B])
        in_maps.append(m)
    res = run_bass_kernel_spmd(nc, in_maps, core_ids=list(range(8)))
    out = np.stack([res.results[b]["y"] for b in range(B)], axis=0)
    return out.astype(np.float32)
```

```python
from contextlib import ExitStack
from concourse.bass_utils import run_bass_kernel_spmd
import numpy as np
import concourse.bass as bass
import concourse.mybir as mybir

F32 = mybir.dt.float32
BF16 = mybir.dt.bfloat16
AF = mybir.ActivationFunctionType
ALU = mybir.AluOpType
AX = mybir.AxisListType


class Region:
    __slots__ = ("name", "last_w", "reads", "dsem")

    def __init__(self, name, dsem=None):
        self.name = name
        self.last_w = None
        self.reads = []
        self.dsem = dsem


class DmaSem:
    def __init__(self, name):
        self.name = name
        self.sem = None
        self.count = 0


class Instr:
    __slots__ = ("eng", "fn", "r", "w", "deps", "is_dma", "dsem", "dma_idx",
                 "milestone", "ms_idx", "dma_before")

    def __init__(self, eng, fn, r, w, is_dma=False, dsem=None):
        self.eng = eng
        self.fn = fn
        self.r = r
        self.w = w
        self.deps = set()
        self.is_dma = is_dma
        self.dsem = dsem
        self.dma_idx = 0
        self.milestone = False
        self.ms_idx = 0
        self.dma_before = {}


ENGS = ("pe", "act", "dve", "pool", "sp")


class Prog:
    def __init__(self, nc, same_engine_sync=True):
        self.nc = nc
        self.instrs = []
        self.same_engine_sync = same_engine_sync
        self.eng_obj = {"pe": nc.tensor, "act": nc.scalar, "dve": nc.vector,
                        "pool": nc.gpsimd, "sp": nc.sync}
        self.dsems = []
        self.finals_l = []
        self._dsem_count_now = {}

    def dsem(self, name):
        d = DmaSem(name)
        self.dsems.append(d)
        return d

    def region(self, name, dsem=None):
        return Region(name, dsem)

    def I(self, eng, fn, r=(), w=()):
        ins = Instr(eng, fn, list(r), list(w))
        self._add(ins)
        return ins

    def dma(self, eng, fn, r=(), w=(), dsem=None):
        if dsem is None:
            for reg in list(w) + list(r):
                if reg.dsem is not None:
                    dsem = reg.dsem
                    break
        assert dsem is not None, "dma needs a DmaSem"
        ins = Instr(eng, fn, list(r), list(w), is_dma=True, dsem=dsem)
        self._add(ins)
        dsem.count += 1
        ins.dma_idx = dsem.count
        return ins

    def _add(self, ins):
        idx = len(self.instrs)
        for reg in ins.r:
            if reg.last_w is not None:
                ins.deps.add(reg.last_w)
        for reg in ins.w:
            if reg.last_w is not None:
                ins.deps.add(reg.last_w)
            for rd in reg.reads:
                ins.deps.add(rd)
        ins.deps.discard(idx)
        for d in ins.deps:
            di = self.instrs[d]
            if di.is_dma:
                ins.dma_before[di.dsem] = di.dsem.count
        for reg in ins.w:
            reg.last_w = idx
            reg.reads = []
        for reg in ins.r:
            if reg.last_w != idx:
                reg.reads.append(idx)
        self.instrs.append(ins)

    def emit(self):
        nc = self.nc
        instrs = self.instrs
        for i, ins in enumerate(instrs):
            for d in ins.deps:
                di = instrs[d]
                if di.is_dma:
                    continue
                if di.eng == ins.eng and (ins.eng == "pe" or not self.same_engine_sync):
                    continue
                di.milestone = True
        cnt = {e: 0 for e in ENGS}
        for ins in instrs:
            if ins.milestone:
                cnt[ins.eng] += 1
                ins.ms_idx = cnt[ins.eng]
        import contextlib
        stack = contextlib.ExitStack()
        sems = {}
        for e in ENGS:
            sems[e] = stack.enter_context(nc.semaphore("s_" + e))
        for d in self.dsems:
            if d.count > 0:
                d.sem = stack.enter_context(nc.semaphore("d_" + d.name))
        seen = {e: {} for e in ENGS}
        nwait = 0
        plan = {e: [] for e in ENGS}
        for ins in instrs:
            need = {}
            for d in ins.deps:
                di = instrs[d]
                if di.is_dma:
                    key = ("d", di.dsem)
                    val = 16 * ins.dma_before[di.dsem]
                    sem = di.dsem.sem
                else:
                    if di.eng == ins.eng and (ins.eng == "pe" or not self.same_engine_sync):
                        continue
                    key = ("e", di.eng)
                    val = di.ms_idx
                    sem = sems[di.eng]
                if need.get(key, (None, 0))[1] < val:
                    need[key] = (sem, val)
            waits = []
            for key, (sem, val) in need.items():
                if seen[ins.eng].get(key, 0) < val:
                    waits.append((sem, val))
                    seen[ins.eng][key] = val
                    nwait += 1
            plan[ins.eng].append((ins, waits))
        finals = {}
        for en, d in self.finals_l:
            finals.setdefault(en, []).append(d)

        def run_stream(ename, eo):
            for ins, waits in plan[ename]:
                for sem, val in waits:
                    eo.wait_ge(sem, val)
                res = ins.fn(eo)
                if ins.is_dma:
                    res.then_inc(ins.dsem.sem, 16)
                elif ins.milestone:
                    res.then_inc(sems[ins.eng], 1)
            for d in finals.get(ename, []):
                eo.wait_ge(d.sem, 16 * d.count)

        with nc.Block() as block:
            @block.tensor
            def _(e):
                run_stream("pe", e)

            @block.scalar
            def _(e):
                run_stream("act", e)

            @block.vector
            def _(e):
                run_stream("dve", e)

            @block.gpsimd
            def _(e):
                run_stream("pool", e)

            @block.sync
            def _(e):
                run_stream("sp", e)
        self.sems = sems
        self.stats = dict(n=len(instrs), waits=nwait, ms=dict(cnt))
        stack.close()

    def final_wait(self, eng, dsem):
        self.finals_l.append((eng, dsem))


D = 1024
NV = 6256
GPRE, GPOST, LNG, LNB, NG, SINK, DTB, ALOG, DSK = 0, 1024, 2048, 3072, 4096, 6144, 6160, 6192, 6224
EPS = 1e-6
NEG = -30000.0
NSLOT = 3

TILES = []
TILES += [("in", 0, 0, 512), ("in", 0, 512, 512)]
TILES += [("kd", 0, 0, 256)]
TILES += [("in", 0, 1152, 128)]
TILES += [("in", 0, 1280 + 512 * i, 512) for i in range(2)]
TILES += [("in", 0, 2304 + 512 * i, 512) for i in range(6)]
TILES += [("in", 0, 5376 + 512 * i, 512) for i in range(4)]
TILES += [("in", 0, 7424 + 512 * i, 512) for i in range(6)]
TILES += [("in", 0, 10496, 32)]
TILES += [("in", 0, 10528 + 512 * i, 512) for i in range(6)]
TILES += [("ba", 0, 0, 512), ("ba", 0, 512, 512)]
TILES += [("bs", 0, 0, 512), ("bs", 0, 512, 512)]
TILES += [("bm", 0, 0, 512), ("bm", 0, 512, 512), ("bm", 1024, 0, 512), ("bm", 1024, 512, 512)]
TILES += [("o", 0, 0, 512), ("o", 0, 512, 512)]
NTILE = len(TILES)


def build(S, depth=2):
    NCH = S // 128
    nc = bass.Bass("TRN2", target_bir_lowering=False)
    dt_in = lambda name, shape, dt=F32: nc.dram_tensor(name, shape, dt, kind="ExternalInput").ap()
    x_d = dt_in("x", [S, D])
    wsrc = {"in": dt_in("w_in", [2, D, 13600]), "kd": dt_in("w_kd", [2, D, 256]),
            "ba": dt_in("w_ba", [2, D, D]), "bs": dt_in("w_bs", [2, D, D]),
            "bm": dt_in("w_bm", [2, 2048, D]), "o": dt_in("w_o", [2, D, D])}
    vecs_d = dt_in("vecs", [2, 128, NV])
    convw_d = dt_in("convw", [2, 128, 24, 5])
    sgb_d = dt_in("sgb", [2, 128, 8])
    sgwT_d = dt_in("sgwT", [2, 128, 8, 128])
    biasT_d = dt_in("biasT", [128, 4, 2, 4, 128])
    consts_d = dt_in("consts", [128, 512])
    y_d = nc.dram_tensor("y", [S, D], F32, kind="ExternalOutput").ap()
    wbf = nc.dram_tensor("wbf", [2, NTILE, 128, 8, 512], BF16, kind="Internal").ap()

    P = Prog(nc)
    es = ExitStack()

    class T:
        def __init__(self, name, shape, dt, psum=False, dma=False):
            if psum:
                self.t = es.enter_context(nc.psum_tensor("p_" + name, shape, dt))
            else:
                self.t = es.enter_context(nc.sbuf_tensor("s_" + name, shape, dt))
            self.r = P.region(name, P.dsem(name) if dma else None)

    cst = T("cst", [128, 512], F32, dma=True)
    identf = cst.t[:, 0:128]; U = cst.t[:, 128:256]; Lm = cst.t[:, 256:384]; ones = cst.t[:, 384:512]
    identb = T("identb", [128, 128], BF16)
    biasT = T("biasT", [128, 4, 2, 4, 128], F32, dma=True)
    vec = [T("vec%d" % l, [128, 6144], BF16, dma=True) for l in range(2)]
    vsm = [T("vsm%d" % l, [128, 112], F32, dma=True) for l in range(2)]
    esink = [T("esink%d" % l, [128, 16], F32) for l in range(2)]
    avec = [T("avec%d" % l, [128, 32], F32) for l in range(2)]
    cw = [T("cw%d" % l, [128, 24, 5], F32, dma=True) for l in range(2)]
    sgb = [T("sgb%d" % l, [128, 8], F32, dma=True) for l in range(2)]
    sgst = T("sgst", [128, 8, 128], F32, dma=True)
    WmT = [T("WmT%d" % l, [128, 8, 128], BF16) for l in range(2)]
    kdp = [T("kdp%d" % l, [128, 2, 128], BF16) for l in range(2)]
    vsb = [T("vsb%d" % l, [128, 2, 2, 65], BF16) for l in range(2)]
    halo = [T("halo%d" % l, [128, 24, 3], F32) for l in range(2)]
    St = [T("S%d" % l, [128, 2048], F32) for l in range(2)]
    Sb = [T("Sb%d" % l, [128, 2048], BF16) for l in range(2)]
    wt = [T("wt%d" % i, [128, 8, 512], BF16, dma=True) for i in range(NSLOT)]
    xa = [T("xa%d" % i, [128, D], F32, dma=True) for i in range(2)]
    x1 = T("x1", [128, D], F32)
    xo = T("xo", [128, D], F32, dma=True)
    sq = T("sq", [128, D], F32)
    ss = T("ss", [128, 16], F32)
    hb = T("hb", [128, D], BF16)
    hT = T("hT", [128, 8, 128], BF16)
    qT = T("qT", [128, 8, 128], BF16)
    kdc = T("kdc", [128, 2, 128], BF16)
    za = T("za", [128, D], BF16)
    scs0 = T("scs", [128, 4, 128], F32); scs = [scs0, scs0]
    PT = T("PT", [128, 2, 4, 128], BF16)
    den = T("den", [128, 8], F32)
    ytm = T("ytm", [128, 2048], BF16)
    yT = T("yT", [128, 32, 128], BF16)
    usb = T("usb", [128, D], F32)
    vss = T("vss", [128, D], F32)
    zs = za
    vn = T("vn", [128, D], BF16)
    zm = T("zm", [128, 2048], BF16)
    stg = T("stg", [128, 131], F32)
    cacc = T("cacc", [128, 128], F32)
    xcT = T("xcT", [128, 24, 128], BF16)
    xtm = T("xtm", [128, 2048], BF16)
    Btm = T("Btm", [128, 4, 128], BF16)
    sm = T("sm", [128, 8, 32], F32)
    AL = T("AL", [128, 8, 128], F32)
    CBm = T("CBm", [128, 4, 128], F32)
    dec4 = T("dec4", [128, 4, 128], F32)
    MT = T("MT", [128, 4, 128], BF16)
    xdt = T("xdt", [128, 2048], BF16)
    xw = T("xw", [128, 2048], BF16)
    yssm = T("yssm", [128, 2048], F32)
    gate = T("gate", [128, 3072], BF16)
    osb = usb; merged = vss; mb = hb; mT = hT; xD = ytm

    class View:
        def __init__(self, base, ap):
            self.t = ap; self.r = base.r
    tmpf = View(sq, sq.t[:, 0:256]); ytmp = View(sq, sq.t[:, 0:512]); mtmp = View(sq, sq.t[:, 512:1024])
    accs = [T("acc%d" % i, [128, 512], F32, psum=True) for i in range(2)]
    pT = T("pT", [128, 8, 128], BF16, psum=True)
    pS = [T("pS%d" % i, [128, 4, 128], F32, psum=True) for i in range(2)]
    pV = T("pV", [128, 512], F32, psum=True)
    pY = [T("pY%d" % i, [128, 512], F32, psum=True) for i in range(2)]

    wbfsem = [P.dsem("wbf%d" % l) for l in range(2)]
    Rwbf = [[P.region("wbf%d_%d" % (l, t), wbfsem[l]) for t in range(NTILE)] for l in range(2)]
    ysem = P.dsem("y")
    Ry = P.region("ydram", ysem)

    I = P.I
    ACT_COPY = AF.Copy

    def acopy(out, in_, r, w, scale=None):
        if scale is None:
            I("act", lambda e: e.activation(out=out, in_=in_, func=ACT_COPY), r=r, w=w)
        else:
            I("act", lambda e: e.activation(out=out, in_=in_, func=ACT_COPY, scale=scale), r=r, w=w)

    def act(out, in_, func, r, w, **kw):
        I("act", lambda e: e.activation(out=out, in_=in_, func=func, **kw), r=r, w=w)

    def tt(out, in0, in1, op, r, w, eng="dve"):
        I(eng, lambda e: e.tensor_tensor(out=out, in0=in0, in1=in1, op=op), r=r, w=w)

    def ts(out, in0, s1, s2, op0, op1, r, w, eng="dve"):
        if op1 is None:
            I(eng, lambda e: e.tensor_scalar(out=out, in0=in0, scalar1=s1, scalar2=None, op0=op0), r=r, w=w)
        else:
            I(eng, lambda e: e.tensor_scalar(out=out, in0=in0, scalar1=s1, scalar2=s2, op0=op0, op1=op1), r=r, w=w)

    def stt(out, in0, scalar, in1, op0, op1, r, w):
        I("dve", lambda e: e.scalar_tensor_tensor(out=out, in0=in0, scalar=scalar, in1=in1, op0=op0, op1=op1), r=r, w=w)

    def mm(out, lhsT, rhs, start, stop, r, w):
        I("pe", lambda e: e.matmul(out, lhsT=lhsT, rhs=rhs, start=start, stop=stop), r=r, w=w)

    def tp(out, in_, r, w):
        I("pe", lambda e: e.transpose(out=out, in_=in_, identity=identb.t[:]), r=r + [identb.r], w=w)

    for l in range(depth):
        for i3 in range(3):
            P.dma("pool", lambda e, l=l, i3=i3: e.dma_start(out=vec[l].t[:, i3 * 2048:(i3 + 1) * 2048],
                                                          in_=vecs_d[l, :, i3 * 2048:(i3 + 1) * 2048]), w=[vec[l].r])
    wtsw = [P.dsem("wtsw%d" % i) for i in range(NSLOT)]
    for l in range(depth):
        for t, (src, row0, col0, n) in enumerate(TILES):
            sap = wsrc[src][l, row0:row0 + 1024, col0:col0 + n].rearrange("(k p) n -> p k n", p=128)
            slot = (l * NTILE + t) % NSLOT
            P.dma("pool", lambda e, slot=slot, n=n, sap=sap: e.dma_start(out=wt[slot].t[:, :, 0:n], in_=sap),
                  w=[wt[slot].r], dsem=wtsw[slot])
            P.dma("sp", lambda e, l=l, t=t, n=n, slot=slot: e.dma_start(out=wbf[l, t, :, :, 0:n], in_=wt[slot].t[:, :, 0:n]),
                  r=[wt[slot].r], w=[Rwbf[l][t]], dsem=Rwbf[l][t].dsem)

    P.dma("sp", lambda e: e.dma_start(out=cst.t[:], in_=consts_d[:, :]), w=[cst.r])
    P.dma("sp", lambda e: e.dma_start(out=biasT.t[:], in_=biasT_d[:, :, :, :, :]), w=[biasT.r])
    I("dve", lambda e: e.tensor_copy(out=identb.t[:], in_=identf), r=[cst.r], w=[identb.r])
    for l in range(depth):
        P.dma("sp", lambda e, l=l: e.dma_start(out=vsm[l].t[:], in_=vecs_d[l, :, 6144:6256]), w=[vsm[l].r])
        P.dma("sp", lambda e, l=l: e.dma_start(out=cw[l].t[:], in_=convw_d[l, :, :, :]), w=[cw[l].r])
        P.dma("sp", lambda e, l=l: e.dma_start(out=sgb[l].t[:], in_=sgb_d[l, :, :]), w=[sgb[l].r])
        P.dma("sp", lambda e, l=l: e.dma_start(out=sgst.t[:], in_=sgwT_d[l, :, :, :]), w=[sgst.r])
        tt(WmT[l].t[:], sgst.t[:], cst.t[:, 128:256].unsqueeze(1).broadcast_to([128, 8, 128]), ALU.mult,
           r=[sgst.r, cst.r], w=[WmT[l].r])
        act(esink[l].t[:], vsm[l].t[:, 0:16], AF.Exp, r=[vsm[l].r], w=[esink[l].r])
        act(avec[l].t[:], vsm[l].t[:, 48:80], AF.Exp, r=[vsm[l].r], w=[avec[l].r])
        ts(avec[l].t[:], avec[l].t[:], -1.0, None, ALU.mult, None, r=[avec[l].r], w=[avec[l].r])
        I("dve", lambda e, l=l: e.memset(kdp[l].t[:], 0.0), w=[kdp[l].r])
        I("dve", lambda e, l=l: e.memset(vsb[l].t[:], 0.0), w=[vsb[l].r])
        I("dve", lambda e, l=l: e.memset(vsb[l].t[:, :, :, 64:65], 1.0), w=[vsb[l].r])
        I("dve", lambda e, l=l: e.memset(halo[l].t[:], 0.0), w=[halo[l].r])
        I("dve", lambda e, l=l: e.memset(St[l].t[:], 0.0), w=[St[l].r])
        I("dve", lambda e, l=l: e.memset(Sb[l].t[:], 0.0), w=[Sb[l].r])

    state = {"slot": 0, "acc": 0}

    def load_w(l, t):
        slot = state["slot"]; state["slot"] = (slot + 1) % NSLOT
        n = TILES[t][3]
        P.dma("sp", lambda e: e.dma_start(out=wt[slot].t[:, :, 0:n], in_=wbf[l, t, :, :, 0:n]),
              r=[Rwbf[l][t]], w=[wt[slot].r])
        return slot

    def next_acc():
        a = accs[state["acc"]]; state["acc"] ^= 1
        return a

    def proj_tm(l, t, consume):
        slot = load_w(l, t); n = TILES[t][3]; a = next_acc()
        for k in range(8):
            mm(a.t[:, 0:n], hT.t[:, k, :], wt[slot].t[:, k, 0:n], k == 0, k == 7, r=[hT.r, wt[slot].r], w=[a.r])
        consume(a, n)

    def proj_fm(l, t, consume):
        slot = load_w(l, t); n = TILES[t][3]
        for j in range(n // 128):
            a = next_acc()
            for k in range(8):
                mm(a.t[:, 0:128], wt[slot].t[:, k, j * 128:(j + 1) * 128], hT.t[:, k, :], k == 0, k == 7,
                   r=[hT.r, wt[slot].r], w=[a.r])
            consume(a, j)

    def transposes(src, nt, dst_ap_fn, r, w):
        for r0 in range(0, nt, 8):
            m = min(8, nt - r0)
            for j in range(m):
                tp(pT.t[:, j, :], src(r0 + j), r=r, w=[pT.r])
            acopy(dst_ap_fn(r0, m), pT.t[:, 0:m, :], r=[pT.r], w=w)

    def rstd_from(col, scale):
        c = ss.t[:, col:col + 1]
        ts(c, c, scale, EPS, ALU.mult, ALU.add, r=[ss.r], w=[ss.r])
        act(c, c, AF.Sqrt, r=[ss.r], w=[ss.r])
        I("dve", lambda e: e.reciprocal(out=c, in_=c), r=[ss.r], w=[ss.r])

    def chunk_layer(l, c, xin, xout):
        V = vec[l]
        act(sq.t[:], xin.t[:], AF.Square, r=[xin.r], w=[sq.r, ss.r], accum_out=ss.t[:, 0:1])
        rstd_from(0, 1.0 / D)
        stt(hb.t[:], xin.t[:], ss.t[:, 0:1], V.t[:, GPRE:GPRE + D], ALU.mult, ALU.mult, r=[xin.r, ss.r, V.r], w=[hb.r])
        transposes(lambda j: hb.t[:, j * 128:(j + 1) * 128], 8, lambda r0, m: hT.t[:, r0:r0 + m, :], r=[hb.r], w=[hT.r])

        for ti in range(2):
            proj_fm(l, ti, lambda a, j, ti=ti: acopy(qT.t[:, ti * 4 + j, :], a.t[:, 0:128], r=[a.r], w=[qT.r], scale=0.125))
        proj_fm(l, 2, lambda a, j: acopy(kdc.t[:, j, :], a.t[:, 0:128], r=[a.r], w=[kdc.r]))
        proj_tm(l, 3, lambda a, n: acopy(vsb[l].t[:, 1, :, 0:64], a.t[:, 0:128].rearrange("p (g d) -> p g d", g=2), r=[a.r], w=[vsb[l].r]))
        for i in range(2):
            proj_tm(l, 4 + i, lambda a, n, i=i: act(za.t[:, i * 512:(i + 1) * 512], a.t[:, :], AF.Silu, r=[a.r], w=[za.r]))
        kbs = [1] if c == 0 else [0, 1]
        sl = slice(2, 4) if c == 0 else slice(0, 4)
        for hq in range(4):
            g = hq // 2
            for rg in range(2):
                r0 = rg * 64
                for kb in kbs:
                    keys = kdp[l] if kb == 0 else kdc
                    for i in range(2):
                        j = 2 * hq + i
                        mm(pS[rg].t[:, kb * 2 + i, :], keys.t[r0:r0 + 64, g, :], qT.t[r0:r0 + 64, j, :], True, True,
                           r=[keys.r, qT.r], w=[pS[rg].r])
                tt(scs[rg].t[:, sl, :], pS[rg].t[:, sl, :], biasT.t[:, hq, rg, sl, :], ALU.add, r=[pS[rg].r, biasT.r], w=[scs[rg].r])
                act(PT.t[:, rg, sl, :], scs[rg].t[:, sl, :], AF.Exp, r=[scs[rg].r], w=[PT.r])
            pVv = pV.t[:, 0:260].rearrange("p (h e) -> p h e", e=65)
            for hh in range(4):
                rg = hh % 2; i = hh // 2
                for kb in kbs:
                    mm(pVv[:, hh, :], PT.t[:, rg, kb * 2 + i, :], vsb[l].t[:, kb, g, :], kb == kbs[0], kb == 1,
                       r=[PT.r, vsb[l].r], w=[pV.r])
            dv = den.t[:, 0:4].unsqueeze(2)
            tt(dv, pVv[:, :, 64:65], esink[l].t[:, 4 * hq:4 * hq + 4].unsqueeze(2), ALU.add, r=[pV.r, esink[l].r], w=[den.r])
            I("dve", lambda e: e.reciprocal(out=den.t[:, 0:4], in_=den.t[:, 0:4]), r=[den.r], w=[den.r])
            tt(tmpf.t[:].rearrange("p (h d) -> p h d", d=64), pVv[:, :, 0:64], dv.broadcast_to([128, 4, 64]), ALU.mult,
               r=[pV.r, den.r], w=[tmpf.r])
            tt(ytm.t[:, hq * 256:(hq + 1) * 256], tmpf.t[:], za.t[:, hq * 256:(hq + 1) * 256], ALU.mult, r=[tmpf.r, za.r], w=[ytm.r])
        transposes(lambda j: ytm.t[:, j * 128:(j + 1) * 128], 8, lambda r0, m: yT.t[:, r0:r0 + m, :], r=[ytm.r], w=[yT.r])
        acopy(kdp[l].t[:], kdc.t[:], r=[kdc.r], w=[kdp[l].r])
        acopy(vsb[l].t[:, 0, :, 0:64], vsb[l].t[:, 1, :, 0:64], r=[vsb[l].r], w=[vsb[l].r])

        for i in range(2):
            proj_tm(l, 6 + i, lambda a, n, i=i: acopy(usb.t[:, i * 512:(i + 1) * 512], a.t[:, :], r=[a.r], w=[usb.r]))
        for i in range(2):
            proj_tm(l, 8 + i, lambda a, n, i=i: acopy(vss.t[:, i * 512:(i + 1) * 512], a.t[:, :], r=[a.r], w=[vss.r]))
        for i in range(2):
            proj_tm(l, 10 + i, lambda a, n, i=i: act(zs.t[:, i * 512:(i + 1) * 512], a.t[:, :], AF.Silu, r=[a.r], w=[zs.r]))
        act(sq.t[:], vss.t[:], AF.Copy, r=[vss.r], w=[sq.r, ss.r], accum_out=ss.t[:, 1:2])
        act(sq.t[:], vss.t[:], AF.Square, r=[vss.r], w=[sq.r, ss.r], accum_out=ss.t[:, 2:3])
        ts(ss.t[:, 3:4], ss.t[:, 1:2], 1.0 / D, None, ALU.mult, None, r=[ss.r], w=[ss.r])
        tt(ss.t[:, 4:5], ss.t[:, 3:4], ss.t[:, 3:4], ALU.mult, r=[ss.r], w=[ss.r])
        ts(ss.t[:, 5:6], ss.t[:, 2:3], 1.0 / D, ss.t[:, 4:5], ALU.mult, ALU.subtract, r=[ss.r], w=[ss.r])
        rstd_from(5, 1.0)
        ts(ss.t[:, 6:7], ss.t[:, 3:4], ss.t[:, 5:6], -1.0, ALU.mult, ALU.mult, r=[ss.r], w=[ss.r])
        act(sq.t[:], vss.t[:], AF.Identity, r=[vss.r, ss.r], w=[sq.r], scale=ss.t[:, 5:6], bias=ss.t[:, 6:7])
        tt(sq.t[:], sq.t[:], V.t[:, LNG:LNG + D], ALU.mult, r=[sq.r, V.r], w=[sq.r])
        tt(vn.t[:], sq.t[:], V.t[:, LNB:LNB + D], ALU.add, r=[sq.r, V.r], w=[vn.r])
        for g in range(8):
            mm(pY[g // 4].t[:, (g % 4) * 128:(g % 4 + 1) * 128], WmT[l].t[:, g, :], vn.t[:, g * 128:(g + 1) * 128], True, True,
               r=[WmT[l].r, vn.r], w=[pY[g // 4].r])
        for hf in range(2):
            cs = slice(hf * 512, (hf + 1) * 512)
            tt(sq.t[:, cs].rearrange("p (g d) -> p g d", d=128), pY[hf].t[:].rearrange("p (g d) -> p g d", d=128),
               sgb[l].t[:, 4 * hf:4 * hf + 4].unsqueeze(2).broadcast_to([128, 4, 128]), ALU.add, r=[pY[hf].r, sgb[l].r], w=[sq.r])
            tt(sq.t[:, cs], sq.t[:, cs], usb.t[:, cs], ALU.mult, r=[sq.r, usb.r], w=[sq.r])
            tt(ytm.t[:, cs], sq.t[:, cs], zs.t[:, cs], ALU.mult, r=[sq.r, zs.r], w=[ytm.r])
        transposes(lambda j: ytm.t[:, j * 128:(j + 1) * 128], 8, lambda r0, m: yT.t[:, 8 + r0:8 + r0 + m, :], r=[ytm.r], w=[yT.r])

        for i in range(4):
            proj_tm(l, 12 + i, lambda a, n, i=i: act(zm.t[:, i * 512:(i + 1) * 512], a.t[:, :], AF.Silu, r=[a.r], w=[zm.r]))

        def conv_consume(a, j, ti):
            jj = ti * 4 + j
            acopy(stg.t[:, 3:131], a.t[:, 0:128], r=[a.r], w=[stg.r])
            acopy(stg.t[:, 0:3], halo[l].t[:, jj, :], r=[halo[l].r], w=[stg.r])
            ts(cacc.t[:], stg.t[:, 0:128], cw[l].t[:, jj, 0:1], cw[l].t[:, jj, 4:5], ALU.mult, ALU.add, r=[stg.r, cw[l].r], w=[cacc.r])
            for k in range(1, 4):
                stt(cacc.t[:], stg.t[:, k:k + 128], cw[l].t[:, jj, k:k + 1], cacc.t[:], ALU.mult, ALU.add, r=[stg.r, cw[l].r, cacc.r], w=[cacc.r])
            act(xcT.t[:, jj, :], cacc.t[:], AF.Silu, r=[cacc.r], w=[xcT.r])
            acopy(halo[l].t[:, jj, :], stg.t[:, 128:131], r=[stg.r], w=[halo[l].r])
        for ti in range(6):
            proj_fm(l, 16 + ti, lambda a, j, ti=ti: conv_consume(a, j, ti))

        def dt_consume(a, n):
            tt(sm.t[:, 0, :], a.t[:, 0:32], vsm[l].t[:, 16:48], ALU.add, r=[a.r, vsm[l].r], w=[sm.r])
            act(sm.t[:, 0, :], sm.t[:, 0, :], AF.Exp, r=[sm.r], w=[sm.r])
            act(sm.t[:, 0, :], sm.t[:, 0, :], AF.Ln, r=[sm.r], w=[sm.r], bias=1.0)
            tt(sm.t[:, 1, :], sm.t[:, 0, :], avec[l].t[:], ALU.mult, r=[sm.r, avec[l].r], w=[sm.r])
        proj_tm(l, 22, dt_consume)
        transposes(lambda j: xcT.t[:, j, :], 16, lambda r0, m: xtm.t[:, r0 * 128:(r0 + m) * 128].rearrange("p (j d) -> p j d", d=128),
                   r=[xcT.r], w=[xtm.r])
        transposes(lambda j: xcT.t[:, 16 + j, :], 4, lambda r0, m: Btm.t[:, 0:4, :], r=[xcT.r], w=[Btm.r])
        mm(pV.t[:, 0:32], U, sm.t[:, 1, :], True, True, r=[cst.r, sm.r], w=[pV.r])
        mm(pV.t[:, 32:64], ones, sm.t[:, 1, :], True, True, r=[cst.r, sm.r], w=[pV.r])
        acopy(sm.t[:, 2, :], pV.t[:, 0:32], r=[pV.r], w=[sm.r])
        acopy(sm.t[:, 4, :], pV.t[:, 32:64], r=[pV.r], w=[sm.r])
        act(sm.t[:, 3, :], sm.t[:, 2, :], AF.Exp, r=[sm.r], w=[sm.r])
        tt(sm.t[:, 5, :], sm.t[:, 4, :], sm.t[:, 2, :], ALU.subtract, r=[sm.r], w=[sm.r])
        act(sm.t[:, 5, :], sm.t[:, 5, :], AF.Exp, r=[sm.r], w=[sm.r])
        act(sm.t[:, 6, :], sm.t[:, 4, :], AF.Exp, r=[sm.r], w=[sm.r])
        v3 = lambda t_, : t_.t[:].rearrange("p (h d) -> p h d", d=64)
        bc = lambda ap: ap.unsqueeze(2).broadcast_to([128, 32, 64])
        tt(v3(xdt), v3(xtm), bc(sm.t[:, 0, :]), ALU.mult, r=[xtm.r, sm.r], w=[xdt.r])
        tt(v3(xD), v3(xtm), bc(vsm[l].t[:, 80:112]), ALU.mult, r=[xtm.r, vsm[l].r], w=[xD.r])
        tt(v3(xw), v3(xdt), bc(sm.t[:, 5, :]), ALU.mult, r=[xdt.r, sm.r], w=[xw.r])
        for g in range(4):
            mm(pS[0].t[:, g, :], xcT.t[:, 16 + g, :], xcT.t[:, 20 + g, :], True, True, r=[xcT.r], w=[pS[0].r])
        tt(CBm.t[:], pS[0].t[:], U.unsqueeze(1).broadcast_to([128, 4, 128]), ALU.mult, r=[pS[0].r, cst.r], w=[CBm.r])
        for g in range(4):
            gs = slice(g * 512, (g + 1) * 512)
            tt(AL.t[:], Lm.unsqueeze(1).broadcast_to([128, 8, 128]), sm.t[:, 1, 8 * g:8 * g + 8].unsqueeze(2).broadcast_to([128, 8, 128]), ALU.mult,
               r=[cst.r, sm.r], w=[AL.r])
            mm(pY[0].t[:], identb.t[:], xD.t[:, gs], True, False, r=[identb.r, xD.r], w=[pY[0].r])
            if c > 0:
                mm(pY[1].t[:], xcT.t[:, 20 + g, :], Sb[l].t[:, gs], True, True, r=[xcT.r, Sb[l].r], w=[pY[1].r])
            for hf in range(2):
                for hh in range(4):
                    h = 8 * g + 4 * hf + hh
                    mm(pS[1].t[:, hh, :], AL.t[:, 4 * hf + hh, :], U, True, True, r=[AL.r, cst.r], w=[pS[1].r])
                act(dec4.t[:], pS[1].t[:], AF.Exp, r=[pS[1].r], w=[dec4.r])
                tt(MT.t[:], dec4.t[:], CBm.t[:, g, :].unsqueeze(1).broadcast_to([128, 4, 128]), ALU.mult, r=[dec4.r, CBm.r], w=[MT.r])
                for hh in range(4):
                    h = 8 * g + 4 * hf + hh
                    mm(pY[0].t[:, (4 * hf + hh) * 64:(4 * hf + hh + 1) * 64], MT.t[:, hh, :], xdt.t[:, h * 64:(h + 1) * 64],
                       False, (hf == 1 and hh == 3), r=[MT.r, xdt.r], w=[pY[0].r])
            if c > 0:
                tt(ytmp.t[:].rearrange("p (h d) -> p h d", d=64), pY[1].t[:].rearrange("p (h d) -> p h d", d=64),
                   sm.t[:, 3, 8 * g:8 * g + 8].unsqueeze(2).broadcast_to([128, 8, 64]), ALU.mult, r=[pY[1].r, sm.r], w=[ytmp.r])
                tt(yssm.t[:, gs], pY[0].t[:], ytmp.t[:], ALU.add, r=[pY[0].r, ytmp.r], w=[yssm.r])
            else:
                acopy(yssm.t[:, gs], pY[0].t[:], r=[pY[0].r], w=[yssm.r])
        for g in range(4):
            gs = slice(g * 512, (g + 1) * 512)
            a = next_acc()
            mm(a.t[:], Btm.t[:, g, :], xw.t[:, gs], True, True, r=[Btm.r, xw.r], w=[a.r])
            if c > 0:
                tt(St[l].t[:, gs].rearrange("p (h d) -> p h d", d=64), St[l].t[:, gs].rearrange("p (h d) -> p h d", d=64),
                   sm.t[:, 6, 8 * g:8 * g + 8].unsqueeze(2).broadcast_to([128, 8, 64]), ALU.mult, r=[St[l].r, sm.r], w=[St[l].r])
                tt(St[l].t[:, gs], St[l].t[:, gs], a.t[:], ALU.add, r=[St[l].r, a.r], w=[St[l].r])
            else:
                acopy(St[l].t[:, gs], a.t[:], r=[a.r], w=[St[l].r])
        acopy(Sb[l].t[:], St[l].t[:], r=[St[l].r], w=[Sb[l].r])
        tt(yssm.t[:], yssm.t[:], zm.t[:], ALU.mult, r=[yssm.r, zm.r], w=[yssm.r])
        for g in range(4):
            act(sq.t[:, 0:512], yssm.t[:, g * 512:(g + 1) * 512], AF.Square, r=[yssm.r], w=[sq.r, ss.r], accum_out=ss.t[:, 8 + g:9 + g])
        ts(ss.t[:, 8:12], ss.t[:, 8:12], 1.0 / 512, EPS, ALU.mult, ALU.add, r=[ss.r], w=[ss.r])
        act(ss.t[:, 8:12], ss.t[:, 8:12], AF.Sqrt, r=[ss.r], w=[ss.r])
        I("dve", lambda e: e.reciprocal(out=ss.t[:, 8:12], in_=ss.t[:, 8:12]), r=[ss.r], w=[ss.r])
        tt(yssm.t[:].rearrange("p (g d) -> p g d", d=512), yssm.t[:].rearrange("p (g d) -> p g d", d=512),
           ss.t[:, 8:12].unsqueeze(2).broadcast_to([128, 4, 512]), ALU.mult, r=[yssm.r, ss.r], w=[yssm.r])
        tt(ytm.t[:], yssm.t[:], V.t[:, NG:NG + 2048], ALU.mult, r=[yssm.r, V.r], w=[ytm.r])
        transposes(lambda j: ytm.t[:, j * 128:(j + 1) * 128], 16, lambda r0, m: yT.t[:, 16 + r0:16 + r0 + m, :], r=[ytm.r], w=[yT.r])

        for i in range(6):
            proj_tm(l, 23 + i, lambda a, n, i=i: act(gate.t[:, i * 512:(i + 1) * 512], a.t[:, :], AF.Sigmoid, r=[a.r], w=[gate.r]))

        for br, (t0, kb0, nk) in enumerate([(29, 0, 1), (31, 8, 1), (33, 16, 2)]):
            for ct in range(2):
                cs = slice(ct * 512, (ct + 1) * 512)
                slots = [load_w(l, t0 + ct + 2 * kh) for kh in range(nk)]
                a = next_acc()
                for kh in range(nk):
                    for k in range(8):
                        mm(a.t[:], yT.t[:, kb0 + kh * 8 + k, :], wt[slots[kh]].t[:, k, :], (kh == 0 and k == 0), (kh == nk - 1 and k == 7),
                           r=[yT.r, wt[slots[kh]].r], w=[a.r])
                gsl = gate.t[:, br * 1024 + ct * 512: br * 1024 + (ct + 1) * 512]
                if br == 0:
                    tt(merged.t[:, cs], a.t[:], gsl, ALU.mult, r=[a.r, gate.r], w=[merged.r])
                else:
                    tt(mtmp.t[:], a.t[:], gsl, ALU.mult, r=[a.r, gate.r], w=[mtmp.r])
                    tt(merged.t[:, cs], merged.t[:, cs], mtmp.t[:], ALU.add, r=[merged.r, mtmp.r], w=[merged.r])
        acopy(mb.t[:], merged.t[:], r=[merged.r], w=[mb.r])
        transposes(lambda j: mb.t[:, j * 128:(j + 1) * 128], 8, lambda r0, m: mT.t[:, r0:r0 + m, :], r=[mb.r], w=[mT.r])
        for ct in range(2):
            slot = load_w(l, 37 + ct)
            a = next_acc()
            for k in range(8):
                mm(a.t[:], mT.t[:, k, :], wt[slot].t[:, k, :], k == 0, k == 7, r=[mT.r, wt[slot].r], w=[a.r])
            acopy(osb.t[:, ct * 512:(ct + 1) * 512], a.t[:], r=[a.r], w=[osb.r])
        act(sq.t[:], osb.t[:], AF.Square, r=[osb.r], w=[sq.r, ss.r], accum_out=ss.t[:, 12:13])
        rstd_from(12, 1.0 / D)
        stt(osb.t[:], osb.t[:], ss.t[:, 12:13], V.t[:, GPOST:GPOST + D], ALU.mult, ALU.mult, r=[osb.r, ss.r, V.r], w=[osb.r])
        tt(xout.t[:], xin.t[:], osb.t[:], ALU.add, r=[xin.r, osb.r], w=[xout.r])

    def load_x(c):
        b = xa[c % 2]
        P.dma("sp", lambda e: e.dma_start(out=b.t[:], in_=x_d[c * 128:(c + 1) * 128, :]), w=[b.r])

    load_x(0)
    for c in range(NCH):
        if c + 1 < NCH:
            load_x(c + 1)
        if depth == 2:
            chunk_layer(0, c, xa[c % 2], x1)
            chunk_layer(1, c, x1, xo)
        else:
            chunk_layer(0, c, xa[c % 2], xo)
        P.dma("sp", lambda e, c=c: e.dma_start(out=y_d[c * 128:(c + 1) * 128, :], in_=xo.t[:]), r=[xo.r], w=[Ry], dsem=ysem)
    P.final_wait("sp", ysem)
    P.emit()
    es.close()
    return nc, P


def _t5_bucket(dist):
    max_exact = 16
    dist_f = np.maximum(dist, 1).astype(np.float32)
    large = max_exact + (np.log(dist_f / np.float32(max_exact)) / np.float32(np.log(128 / max_exact))
                         * np.float32(32 - max_exact)).astype(np.int32)
    large = np.minimum(large, 31)
    return np.where(dist < max_exact, dist, large)


def host_prep(inputs, S):
    f = lambda a: np.ascontiguousarray(np.asarray(a, dtype=np.float32))
    w_in = f(inputs["w_in"])
    kcols = w_in[:, :, 1024:1152]
    w_kd = np.concatenate([kcols[:, :, 0:64], kcols[:, :, 0:64], kcols[:, :, 64:128], kcols[:, :, 64:128]], axis=2)
    vec = np.concatenate([f(inputs["norm_pre"]), f(inputs["norm_post"]), f(inputs["sg_ln_g"]), f(inputs["sg_ln_b"]),
                          f(inputs["ssm_norm_g"]), f(inputs["att_sinks"]), f(inputs["ssm_dt_bias"]), f(inputs["ssm_a_log"]),
                          f(inputs["ssm_d"])], axis=1)
    vecs = np.ascontiguousarray(np.broadcast_to(vec[:, None, :], (2, 128, NV)))
    cwf = np.concatenate([f(inputs["ssm_conv_w"]), f(inputs["ssm_conv_b"])[:, None, :]], axis=1)
    convw = np.ascontiguousarray(cwf.reshape(2, 5, 24, 128).transpose(0, 3, 2, 1))
    sgb = np.ascontiguousarray(f(inputs["sg_b"]).transpose(0, 2, 1))
    sgwT = np.ascontiguousarray(f(inputs["sg_w"]).transpose(0, 3, 1, 2))
    rel = f(inputs["rel_bias"])
    table = np.concatenate([rel, np.full((1, 16), NEG, np.float32)], axis=0)
    s = np.arange(128)[:, None]; q = np.arange(128)[None, :]
    d0 = q + 128 - s; valid0 = (d0 >= 0) & (d0 < 128)
    d1 = q - s; valid1 = d1 >= 0
    idx0 = np.where(valid0, _t5_bucket(np.maximum(d0, 0)), 32)
    idx1 = np.where(valid1, _t5_bucket(np.maximum(d1, 0)), 32)
    idx = np.stack([idx0, idx1], axis=1)
    bt = table[idx].transpose(0, 1, 3, 2)
    bt = bt.reshape(128, 2, 4, 2, 2, 128)
    biasT = np.ascontiguousarray(bt.transpose(0, 2, 4, 1, 3, 5).reshape(128, 4, 2, 4, 128))
    k = np.arange(128)[:, None]; j = np.arange(128)[None, :]
    consts = np.concatenate([(k == j), (k <= j), (k > j), np.ones((128, 128), bool)], axis=1).astype(np.float32)
    common = {"w_in": w_in, "w_kd": np.ascontiguousarray(w_kd), "w_ba": f(inputs["w_br_att"]), "w_bs": f(inputs["w_br_sg"]),
              "w_bm": f(inputs["w_br_ssm"]), "w_o": f(inputs["w_out"]), "vecs": vecs, "convw": convw, "sgb": sgb,
              "sgwT": sgwT, "biasT": biasT, "consts": np.ascontiguousarray(consts)}
    return common


_CACHE = {}


def kernel(**inputs):
    x = np.asarray(inputs["x"], dtype=np.float32)
    B, S, _ = x.shape
    if S not in _CACHE:
        _CACHE[S] = build(S)[0]
    nc = _CACHE[S]
    common = host_prep(inputs, S)
    in_maps = []
    for core in range(B):
        m = dict(common)
        m["x"] = np.ascontiguousarray(x[core])
        in_maps.append(m)
    res = run_bass_kernel_spmd(nc, in_maps, core_ids=list(range(B)))
    out = np.stack([res.results[b]["y"] for b in range(B)], axis=0)
    return out.astype(np.float32)
```

```python
from contextlib import ExitStack
from concourse.bass_utils import run_bass_kernel_spmd
import numpy as np
import concourse.bass as bass
import concourse.mybir as mybir

F32 = mybir.dt.float32
BF16 = mybir.dt.bfloat16
AF = mybir.ActivationFunctionType
ALU = mybir.AluOpType
AX = mybir.AxisListType


class Region:
    __slots__ = ("name", "last_w", "reads", "dsem")

    def __init__(self, name, dsem=None):
        self.name = name
        self.last_w = None
        self.reads = []
        self.dsem = dsem


class DmaSem:
    def __init__(self, name):
        self.name = name
        self.sem = None
        self.count = 0


class Instr:
    __slots__ = ("eng", "fn", "r", "w", "deps", "is_dma", "dsem", "dma_idx",
                 "milestone", "ms_idx", "dma_before")

    def __init__(self, eng, fn, r, w, is_dma=False, dsem=None):
        self.eng = eng
        self.fn = fn
        self.r = r
        self.w = w
        self.deps = set()
        self.is_dma = is_dma
        self.dsem = dsem
        self.dma_idx = 0
        self.milestone = False
        self.ms_idx = 0
        self.dma_before = {}


ENGS = ("pe", "act", "dve", "pool", "sp")


class Prog:
    def __init__(self, nc, same_engine_sync=True):
        self.nc = nc
        self.instrs = []
        self.same_engine_sync = same_engine_sync
        self.eng_obj = {"pe": nc.tensor, "act": nc.scalar, "dve": nc.vector,
                        "pool": nc.gpsimd, "sp": nc.sync}
        self.dsems = []
        self.finals_l = []
        self._dsem_count_now = {}

    def dsem(self, name):
        d = DmaSem(name)
        self.dsems.append(d)
        return d

    def region(self, name, dsem=None):
        return Region(name, dsem)

    def I(self, eng, fn, r=(), w=()):
        ins = Instr(eng, fn, list(r), list(w))
        self._add(ins)
        return ins

    def dma(self, eng, fn, r=(), w=(), dsem=None):
        if dsem is None:
            for reg in list(w) + list(r):
                if reg.dsem is not None:
                    dsem = reg.dsem
                    break
        assert dsem is not None, "dma needs a DmaSem"
        ins = Instr(eng, fn, list(r), list(w), is_dma=True, dsem=dsem)
        self._add(ins)
        dsem.count += 1
        ins.dma_idx = dsem.count
        return ins

    def _add(self, ins):
        idx = len(self.instrs)
        for reg in ins.r:
            if reg.last_w is not None:
                ins.deps.add(reg.last_w)
        for reg in ins.w:
            if reg.last_w is not None:
                ins.deps.add(reg.last_w)
            for rd in reg.reads:
                ins.deps.add(rd)
        ins.deps.discard(idx)
        for d in ins.deps:
            di = self.instrs[d]
            if di.is_dma:
                ins.dma_before[di.dsem] = di.dsem.count
        for reg in ins.w:
            reg.last_w = idx
            reg.reads = []
        for reg in ins.r:
            if reg.last_w != idx:
                reg.reads.append(idx)
        self.instrs.append(ins)

    def emit(self):
        nc = self.nc
        instrs = self.instrs
        for i, ins in enumerate(instrs):
            for d in ins.deps:
                di = instrs[d]
                if di.is_dma:
                    continue
                if di.eng == ins.eng and (ins.eng == "pe" or not self.same_engine_sync):
                    continue
                di.milestone = True
        cnt = {e: 0 for e in ENGS}
        for ins in instrs:
            if ins.milestone:
                cnt[ins.eng] += 1
                ins.ms_idx = cnt[ins.eng]
        import contextlib
        stack = contextlib.ExitStack()
        sems = {}
        for e in ENGS:
            sems[e] = stack.enter_context(nc.semaphore("s_" + e))
        for d in self.dsems:
            if d.count > 0:
                d.sem = stack.enter_context(nc.semaphore("d_" + d.name))
        seen = {e: {} for e in ENGS}
        nwait = 0
        plan = {e: [] for e in ENGS}
        for ins in instrs:
            need = {}
            for d in ins.deps:
                di = instrs[d]
                if di.is_dma:
                    key = ("d", di.dsem)
                    val = 16 * ins.dma_before[di.dsem]
                    sem = di.dsem.sem
                else:
                    if di.eng == ins.eng and (ins.eng == "pe" or not self.same_engine_sync):
                        continue
                    key = ("e", di.eng)
                    val = di.ms_idx
                    sem = sems[di.eng]
                if need.get(key, (None, 0))[1] < val:
                    need[key] = (sem, val)
            waits = []
            for key, (sem, val) in need.items():
                if seen[ins.eng].get(key, 0) < val:
                    waits.append((sem, val))
                    seen[ins.eng][key] = val
                    nwait += 1
            plan[ins.eng].append((ins, waits))
        finals = {}
        for en, d in self.finals_l:
            finals.setdefault(en, []).append(d)

        def run_stream(ename, eo):
            for ins, waits in plan[ename]:
                for sem, val in waits:
                    eo.wait_ge(sem, val)
                res = ins.fn(eo)
                if ins.is_dma:
                    res.then_inc(ins.dsem.sem, 16)
                elif ins.milestone:
                    res.then_inc(sems[ins.eng], 1)
            for d in finals.get(ename, []):
                eo.wait_ge(d.sem, 16 * d.count)

        with nc.Block() as block:
            @block.tensor
            def _(e):
                run_stream("pe", e)

            @block.scalar
            def _(e):
                run_stream("act", e)

            @block.vector
            def _(e):
                run_stream("dve", e)

            @block.gpsimd
            def _(e):
                run_stream("pool", e)

            @block.sync
            def _(e):
                run_stream("sp", e)
        self.sems = sems
        self.stats = dict(n=len(instrs), waits=nwait, ms=dict(cnt))
        stack.close()

    def final_wait(self, eng, dsem):
        self.finals_l.append((eng, dsem))


D = 1024
NV = 6256
GPRE, GPOST, LNG, LNB, NG, SINK, DTB, ALOG, DSK = 0, 1024, 2048, 3072, 4096, 6144, 6160, 6192, 6224
EPS = 1e-6
NEG = -30000.0
NSLOT = 3

TILES = []
TILES += [("in", 0, 0, 512), ("in", 0, 512, 512)]
TILES += [("kd", 0, 0, 256)]
TILES += [("in", 0, 1152, 128)]
TILES += [("in", 0, 1280 + 512 * i, 512) for i in range(2)]
TILES += [("in", 0, 2304 + 512 * i, 512) for i in range(6)]
TILES += [("in", 0, 5376 + 512 * i, 512) for i in range(4)]
TILES += [("in", 0, 7424 + 512 * i, 512) for i in range(6)]
TILES += [("in", 0, 10496, 32)]
TILES += [("in", 0, 10528 + 512 * i, 512) for i in range(6)]
TILES += [("ba", 0, 0, 512), ("ba", 0, 512, 512)]
TILES += [("bs", 0, 0, 512), ("bs", 0, 512, 512)]
TILES += [("bm", 0, 0, 512), ("bm", 0, 512, 512), ("bm", 1024, 0, 512), ("bm", 1024, 512, 512)]
TILES += [("o", 0, 0, 512), ("o", 0, 512, 512)]
NTILE = len(TILES)


def build(S, depth=2):
    NCH = S // 128
    nc = bass.Bass("TRN2", target_bir_lowering=False)
    dt_in = lambda name, shape, dt=F32: nc.dram_tensor(name, shape, dt, kind="ExternalInput").ap()
    x_d = dt_in("x", [S, D])
    wsrc = {"in": dt_in("w_in", [2, D, 13600]), "kd": dt_in("w_kd", [2, D, 256]),
            "ba": dt_in("w_ba", [2, D, D]), "bs": dt_in("w_bs", [2, D, D]),
            "bm": dt_in("w_bm", [2, 2048, D]), "o": dt_in("w_o", [2, D, D])}
    vecs_d = dt_in("vecs", [2, 128, NV])
    convw_d = dt_in("convw", [2, 128, 24, 5])
    sgb_d = dt_in("sgb", [2, 128, 8])
    sgwT_d = dt_in("sgwT", [2, 128, 8, 128])
    biasT_d = dt_in("biasT", [128, 4, 2, 4, 128])
    consts_d = dt_in("consts", [128, 512])
    y_d = nc.dram_tensor("y", [S, D], F32, kind="ExternalOutput").ap()
    wbf = nc.dram_tensor("wbf", [2, NTILE, 128, 8, 512], BF16, kind="Internal").ap()

    P = Prog(nc)
    es = ExitStack()

    class T:
        def __init__(self, name, shape, dt, psum=False, dma=False):
            if psum:
                self.t = es.enter_context(nc.psum_tensor("p_" + name, shape, dt))
            else:
                self.t = es.enter_context(nc.sbuf_tensor("s_" + name, shape, dt))
            self.r = P.region(name, P.dsem(name) if dma else None)

    cst = T("cst", [128, 512], F32, dma=True)
    identf = cst.t[:, 0:128]; U = cst.t[:, 128:256]; Lm = cst.t[:, 256:384]; ones = cst.t[:, 384:512]
    identb = T("identb", [128, 128], BF16)
    biasT = T("biasT", [128, 4, 2, 4, 128], F32, dma=True)
    vec = [T("vec%d" % l, [128, 6144], BF16, dma=True) for l in range(2)]
    vsm = [T("vsm%d" % l, [128, 112], F32, dma=True) for l in range(2)]
    esink = [T("esink%d" % l, [128, 16], F32) for l in range(2)]
    avec = [T("avec%d" % l, [128, 32], F32) for l in range(2)]
    cw = [T("cw%d" % l, [128, 24, 5], F32, dma=True) for l in range(2)]
    sgb = [T("sgb%d" % l, [128, 8], F32, dma=True) for l in range(2)]
    sgst = T("sgst", [128, 8, 128], F32, dma=True)
    WmT = [T("WmT%d" % l, [128, 8, 128], BF16) for l in range(2)]
    kdp = [T("kdp%d" % l, [128, 2, 128], BF16) for l in range(2)]
    vsb = [T("vsb%d" % l, [128, 2, 2, 65], BF16) for l in range(2)]
    halo = [T("halo%d" % l, [128, 24, 3], F32) for l in range(2)]
    St = [T("S%d" % l, [128, 2048], F32) for l in range(2)]
    Sb = [T("Sb%d" % l, [128, 2048], BF16) for l in range(2)]
    wt = [T("wt%d" % i, [128, 8, 512], BF16, dma=True) for i in range(NSLOT)]
    xa = [T("xa%d" % i, [128, D], F32, dma=True) for i in range(2)]
    x1 = T("x1", [128, D], F32)
    xo = T("xo", [128, D], F32, dma=True)
    sq = T("sq", [128, D], F32)
    ss = T("ss", [128, 16], F32)
    hb = T("hb", [128, D], BF16)
    hT = T("hT", [128, 8, 128], BF16)
    qT = T("qT", [128, 8, 128], BF16)
    kdc = T("kdc", [128, 2, 128], BF16)
    za = T("za", [128, D], BF16)
    scs0 = T("scs", [128, 4, 128], F32); scs = [scs0, scs0]
    PT = T("PT", [128, 2, 4, 128], BF16)
    den = T("den", [128, 8], F32)
    ytm = T("ytm", [128, 2048], BF16)
    yT = T("yT", [128, 32, 128], BF16)
    usb = T("usb", [128, D], F32)
    vss = T("vss", [128, D], F32)
    zs = za
    vn = T("vn", [128, D], BF16)
    zm = T("zm", [128, 2048], BF16)
    stg = T("stg", [128, 131], F32)
    cacc = T("cacc", [128, 128], F32)
    xcT = T("xcT", [128, 24, 128], BF16)
    xtm = T("xtm", [128, 2048], BF16)
    Btm = T("Btm", [128, 4, 128], BF16)
    sm = T("sm", [128, 8, 32], F32)
    AL = T("AL", [128, 8, 128], F32)
    CBm = T("CBm", [128, 4, 128], F32)
    dec4 = T("dec4", [128, 4, 128], F32)
    MT = T("MT", [128, 4, 128], BF16)
    xdt = T("xdt", [128, 2048], BF16)
    xw = T("xw", [128, 2048], BF16)
    yssm = T("yssm", [128, 2048], F32)
    gate = T("gate", [128, 3072], BF16)
    osb = usb; merged = vss; mb = hb; mT = hT; xD = ytm

    class View:
        def __init__(self, base, ap):
            self.t = ap; self.r = base.r
    tmpf = View(sq, sq.t[:, 0:256]); ytmp = View(sq, sq.t[:, 0:512]); mtmp = View(sq, sq.t[:, 512:1024])
    accs = [T("acc%d" % i, [128, 512], F32, psum=True) for i in range(2)]
    pT = T("pT", [128, 8, 128], BF16, psum=True)
    pS = [T("pS%d" % i, [128, 4, 128], F32, psum=True) for i in range(2)]
    pV = T("pV", [128, 512], F32, psum=True)
    pY = [T("pY%d" % i, [128, 512], F32, psum=True) for i in range(2)]

    wbfsem = [P.dsem("wbf%d" % l) for l in range(2)]
    Rwbf = [[P.region("wbf%d_%d" % (l, t), wbfsem[l]) for t in range(NTILE)] for l in range(2)]
    ysem = P.dsem("y")
    Ry = P.region("ydram", ysem)

    I = P.I
    ACT_COPY = AF.Copy

    def acopy(out, in_, r, w, scale=None):
        if scale is None:
            I("act", lambda e: e.activation(out=out, in_=in_, func=ACT_COPY), r=r, w=w)
        else:
            I("act", lambda e: e.activation(out=out, in_=in_, func=ACT_COPY, scale=scale), r=r, w=w)

    def act(out, in_, func, r, w, **kw):
        I("act", lambda e: e.activation(out=out, in_=in_, func=func, **kw), r=r, w=w)

    def tt(out, in0, in1, op, r, w, eng="dve"):
        I(eng, lambda e: e.tensor_tensor(out=out, in0=in0, in1=in1, op=op), r=r, w=w)

    def ts(out, in0, s1, s2, op0, op1, r, w, eng="dve"):
        if op1 is None:
            I(eng, lambda e: e.tensor_scalar(out=out, in0=in0, scalar1=s1, scalar2=None, op0=op0), r=r, w=w)
        else:
            I(eng, lambda e: e.tensor_scalar(out=out, in0=in0, scalar1=s1, scalar2=s2, op0=op0, op1=op1), r=r, w=w)

    def stt(out, in0, scalar, in1, op0, op1, r, w):
        I("dve", lambda e: e.scalar_tensor_tensor(out=out, in0=in0, scalar=scalar, in1=in1, op0=op0, op1=op1), r=r, w=w)

    def mm(out, lhsT, rhs, start, stop, r, w):
        I("pe", lambda e: e.matmul(out, lhsT=lhsT, rhs=rhs, start=start, stop=stop), r=r, w=w)

    def tp(out, in_, r, w):
        I("pe", lambda e: e.transpose(out=out, in_=in_, identity=identb.t[:]), r=r + [identb.r], w=w)

    for l in range(depth):
        for i3 in range(3):
            P.dma("pool", lambda e, l=l, i3=i3: e.dma_start(out=vec[l].t[:, i3 * 2048:(i3 + 1) * 2048],
                                                          in_=vecs_d[l, :, i3 * 2048:(i3 + 1) * 2048]), w=[vec[l].r])
    wtsw = [P.dsem("wtsw%d" % i) for i in range(NSLOT)]
    for l in range(depth):
        for t, (src, row0, col0, n) in enumerate(TILES):
            sap = wsrc[src][l, row0:row0 + 1024, col0:col0 + n].rearrange("(k p) n -> p k n", p=128)
            slot = (l * NTILE + t) % NSLOT
            P.dma("pool", lambda e, slot=slot, n=n, sap=sap: e.dma_start(out=wt[slot].t[:, :, 0:n], in_=sap),
                  w=[wt[slot].r], dsem=wtsw[slot])
            P.dma("sp", lambda e, l=l, t=t, n=n, slot=slot: e.dma_start(out=wbf[l, t, :, :, 0:n], in_=wt[slot].t[:, :, 0:n]),
                  r=[wt[slot].r], w=[Rwbf[l][t]], dsem=Rwbf[l][t].dsem)

    P.dma("sp", lambda e: e.dma_start(out=cst.t[:], in_=consts_d[:, :]), w=[cst.r])
    P.dma("sp", lambda e: e.dma_start(out=biasT.t[:], in_=biasT_d[:, :, :, :, :]), w=[biasT.r])
    I("dve", lambda e: e.tensor_copy(out=identb.t[:], in_=identf), r=[cst.r], w=[identb.r])
    for l in range(depth):
        P.dma("sp", lambda e, l=l: e.dma_start(out=vsm[l].t[:], in_=vecs_d[l, :, 6144:6256]), w=[vsm[l].r])
        P.dma("sp", lambda e, l=l: e.dma_start(out=cw[l].t[:], in_=convw_d[l, :, :, :]), w=[cw[l].r])
        P.dma("sp", lambda e, l=l: e.dma_start(out=sgb[l].t[:], in_=sgb_d[l, :, :]), w=[sgb[l].r])
        P.dma("sp", lambda e, l=l: e.dma_start(out=sgst.t[:], in_=sgwT_d[l, :, :, :]), w=[sgst.r])
        tt(WmT[l].t[:], sgst.t[:], cst.t[:, 128:256].unsqueeze(1).broadcast_to([128, 8, 128]), ALU.mult,
           r=[sgst.r, cst.r], w=[WmT[l].r])
        act(esink[l].t[:], vsm[l].t[:, 0:16], AF.Exp, r=[vsm[l].r], w=[esink[l].r])
        act(avec[l].t[:], vsm[l].t[:, 48:80], AF.Exp, r=[vsm[l].r], w=[avec[l].r])
        ts(avec[l].t[:], avec[l].t[:], -1.0, None, ALU.mult, None, r=[avec[l].r], w=[avec[l].r])
        I("dve", lambda e, l=l: e.memset(kdp[l].t[:], 0.0), w=[kdp[l].r])
        I("dve", lambda e, l=l: e.memset(vsb[l].t[:], 0.0), w=[vsb[l].r])
        I("dve", lambda e, l=l: e.memset(vsb[l].t[:, :, :, 64:65], 1.0), w=[vsb[l].r])
        I("dve", lambda e, l=l: e.memset(halo[l].t[:], 0.0), w=[halo[l].r])
        I("dve", lambda e, l=l: e.memset(St[l].t[:], 0.0), w=[St[l].r])
        I("dve", lambda e, l=l: e.memset(Sb[l].t[:], 0.0), w=[Sb[l].r])

    state = {"slot": 0, "acc": 0}

    def load_w(l, t):
        slot = state["slot"]; state["slot"] = (slot + 1) % NSLOT
        n = TILES[t][3]
        P.dma("sp", lambda e: e.dma_start(out=wt[slot].t[:, :, 0:n], in_=wbf[l, t, :, :, 0:n]),
              r=[Rwbf[l][t]], w=[wt[slot].r])
        return slot

    def next_acc():
        a = accs[state["acc"]]; state["acc"] ^= 1
        return a

    def proj_tm(l, t, consume):
        slot = load_w(l, t); n = TILES[t][3]; a = next_acc()
        for k in range(8):
            mm(a.t[:, 0:n], hT.t[:, k, :], wt[slot].t[:, k, 0:n], k == 0, k == 7, r=[hT.r, wt[slot].r], w=[a.r])
        consume(a, n)

    def proj_fm(l, t, consume):
        slot = load_w(l, t); n = TILES[t][3]
        for j in range(n // 128):
            a = next_acc()
            for k in range(8):
                mm(a.t[:, 0:128], wt[slot].t[:, k, j * 128:(j + 1) * 128], hT.t[:, k, :], k == 0, k == 7,
                   r=[hT.r, wt[slot].r], w=[a.r])
            consume(a, j)

    def transposes(src, nt, dst_ap_fn, r, w):
        for r0 in range(0, nt, 8):
            m = min(8, nt - r0)
            for j in range(m):
                tp(pT.t[:, j, :], src(r0 + j), r=r, w=[pT.r])
            acopy(dst_ap_fn(r0, m), pT.t[:, 0:m, :], r=[pT.r], w=w)

    def rstd_from(col, scale):
        c = ss.t[:, col:col + 1]
        ts(c, c, scale, EPS, ALU.mult, ALU.add, r=[ss.r], w=[ss.r])
        act(c, c, AF.Sqrt, r=[ss.r], w=[ss.r])
        I("dve", lambda e: e.reciprocal(out=c, in_=c), r=[ss.r], w=[ss.r])

    pending = []

    def pump(n=None):
        k = len(pending) if n is None else min(n, len(pending))
        for _ in range(k):
            pending.pop(0)()

    def chunk_layer(l, c, xin, xout):
        V = vec[l]
        def conv_consume(a, j, ti):
            jj = ti * 4 + j
            acopy(stg.t[:, 3:131], a.t[:, 0:128], r=[a.r], w=[stg.r])
            acopy(stg.t[:, 0:3], halo[l].t[:, jj, :], r=[halo[l].r], w=[stg.r])
            ts(cacc.t[:], stg.t[:, 0:128], cw[l].t[:, jj, 0:1], cw[l].t[:, jj, 4:5], ALU.mult, ALU.add, r=[stg.r, cw[l].r], w=[cacc.r])
            for k in range(1, 4):
                stt(cacc.t[:], stg.t[:, k:k + 128], cw[l].t[:, jj, k:k + 1], cacc.t[:], ALU.mult, ALU.add, r=[stg.r, cw[l].r, cacc.r], w=[cacc.r])
            act(xcT.t[:, jj, :], cacc.t[:], AF.Silu, r=[cacc.r], w=[xcT.r])
            acopy(halo[l].t[:, jj, :], stg.t[:, 128:131], r=[stg.r], w=[halo[l].r])

        def dt_consume(a, n):
            tt(sm.t[:, 0, :], a.t[:, 0:32], vsm[l].t[:, 16:48], ALU.add, r=[a.r, vsm[l].r], w=[sm.r])
            act(sm.t[:, 0, :], sm.t[:, 0, :], AF.Exp, r=[sm.r], w=[sm.r])
            act(sm.t[:, 0, :], sm.t[:, 0, :], AF.Ln, r=[sm.r], w=[sm.r], bias=1.0)
            tt(sm.t[:, 1, :], sm.t[:, 0, :], avec[l].t[:], ALU.mult, r=[sm.r, avec[l].r], w=[sm.r])

        act(sq.t[:], xin.t[:], AF.Square, r=[xin.r], w=[sq.r, ss.r], accum_out=ss.t[:, 0:1])
        rstd_from(0, 1.0 / D)
        stt(hb.t[:], xin.t[:], ss.t[:, 0:1], V.t[:, GPRE:GPRE + D], ALU.mult, ALU.mult, r=[xin.r, ss.r, V.r], w=[hb.r])
        transposes(lambda j: hb.t[:, j * 128:(j + 1) * 128], 8, lambda r0, m: hT.t[:, r0:r0 + m, :], r=[hb.r], w=[hT.r])

        for ti in range(2):
            proj_fm(l, ti, lambda a, j, ti=ti: acopy(qT.t[:, ti * 4 + j, :], a.t[:, 0:128], r=[a.r], w=[qT.r], scale=0.125))
        proj_fm(l, 2, lambda a, j: acopy(kdc.t[:, j, :], a.t[:, 0:128], r=[a.r], w=[kdc.r]))
        proj_tm(l, 3, lambda a, n: acopy(vsb[l].t[:, 1, :, 0:64], a.t[:, 0:128].rearrange("p (g d) -> p g d", g=2), r=[a.r], w=[vsb[l].r]))
        for i in range(2):
            proj_tm(l, 4 + i, lambda a, n, i=i: act(za.t[:, i * 512:(i + 1) * 512], a.t[:, :], AF.Silu, r=[a.r], w=[za.r]))
        kbs = [1] if c == 0 else [0, 1]
        sl = slice(2, 4) if c == 0 else slice(0, 4)
        for i in range(2):
            pending.append(lambda i=i: proj_tm(l, 6 + i, lambda a, n, i=i: acopy(usb.t[:, i * 512:(i + 1) * 512], a.t[:, :], r=[a.r], w=[usb.r])))
        for i in range(2):
            pending.append(lambda i=i: proj_tm(l, 8 + i, lambda a, n, i=i: acopy(vss.t[:, i * 512:(i + 1) * 512], a.t[:, :], r=[a.r], w=[vss.r])))
        for hq in range(4):
            g = hq // 2
            for rg in range(2):
                r0 = rg * 64
                for kb in kbs:
                    keys = kdp[l] if kb == 0 else kdc
                    for i in range(2):
                        j = 2 * hq + i
                        mm(pS[rg].t[:, kb * 2 + i, :], keys.t[r0:r0 + 64, g, :], qT.t[r0:r0 + 64, j, :], True, True,
                           r=[keys.r, qT.r], w=[pS[rg].r])
                tt(scs[rg].t[:, sl, :], pS[rg].t[:, sl, :], biasT.t[:, hq, rg, sl, :], ALU.add, r=[pS[rg].r, biasT.r], w=[scs[rg].r])
                act(PT.t[:, rg, sl, :], scs[rg].t[:, sl, :], AF.Exp, r=[scs[rg].r], w=[PT.r])
            pVv = pV.t[:, 0:260].rearrange("p (h e) -> p h e", e=65)
            for hh in range(4):
                rg = hh % 2; i = hh // 2
                for kb in kbs:
                    mm(pVv[:, hh, :], PT.t[:, rg, kb * 2 + i, :], vsb[l].t[:, kb, g, :], kb == kbs[0], kb == 1,
                       r=[PT.r, vsb[l].r], w=[pV.r])
            dv = den.t[:, 0:4].unsqueeze(2)
            tt(dv, pVv[:, :, 64:65], esink[l].t[:, 4 * hq:4 * hq + 4].unsqueeze(2), ALU.add, r=[pV.r, esink[l].r], w=[den.r])
            I("dve", lambda e: e.reciprocal(out=den.t[:, 0:4], in_=den.t[:, 0:4]), r=[den.r], w=[den.r])
            tt(tmpf.t[:].rearrange("p (h d) -> p h d", d=64), pVv[:, :, 0:64], dv.broadcast_to([128, 4, 64]), ALU.mult,
               r=[pV.r, den.r], w=[tmpf.r])
            tt(ytm.t[:, hq * 256:(hq + 1) * 256], tmpf.t[:], za.t[:, hq * 256:(hq + 1) * 256], ALU.mult, r=[tmpf.r, za.r], w=[ytm.r])
            pump(1)
        transposes(lambda j: ytm.t[:, j * 128:(j + 1) * 128], 8, lambda r0, m: yT.t[:, r0:r0 + m, :], r=[ytm.r], w=[yT.r])
        acopy(kdp[l].t[:], kdc.t[:], r=[kdc.r], w=[kdp[l].r])
        acopy(vsb[l].t[:, 0, :, 0:64], vsb[l].t[:, 1, :, 0:64], r=[vsb[l].r], w=[vsb[l].r])

        pump()
        for i in range(2):
            proj_tm(l, 10 + i, lambda a, n, i=i: act(zs.t[:, i * 512:(i + 1) * 512], a.t[:, :], AF.Silu, r=[a.r], w=[zs.r]))
        for i in range(4):
            pending.append(lambda i=i: proj_tm(l, 12 + i, lambda a, n, i=i: act(zm.t[:, i * 512:(i + 1) * 512], a.t[:, :], AF.Silu, r=[a.r], w=[zm.r])))
        for ti in range(6):
            pending.append(lambda ti=ti: proj_fm(l, 16 + ti, lambda a, j, ti=ti: conv_consume(a, j, ti)))
        pending.append(lambda: proj_tm(l, 22, dt_consume))
        act(sq.t[:], vss.t[:], AF.Copy, r=[vss.r], w=[sq.r, ss.r], accum_out=ss.t[:, 1:2])
        act(sq.t[:], vss.t[:], AF.Square, r=[vss.r], w=[sq.r, ss.r], accum_out=ss.t[:, 2:3])
        ts(ss.t[:, 3:4], ss.t[:, 1:2], 1.0 / D, None, ALU.mult, None, r=[ss.r], w=[ss.r])
        tt(ss.t[:, 4:5], ss.t[:, 3:4], ss.t[:, 3:4], ALU.mult, r=[ss.r], w=[ss.r])
        ts(ss.t[:, 5:6], ss.t[:, 2:3], 1.0 / D, ss.t[:, 4:5], ALU.mult, ALU.subtract, r=[ss.r], w=[ss.r])
        rstd_from(5, 1.0)
        ts(ss.t[:, 6:7], ss.t[:, 3:4], ss.t[:, 5:6], -1.0, ALU.mult, ALU.mult, r=[ss.r], w=[ss.r])
        act(sq.t[:], vss.t[:], AF.Identity, r=[vss.r, ss.r], w=[sq.r], scale=ss.t[:, 5:6], bias=ss.t[:, 6:7])
        tt(sq.t[:], sq.t[:], V.t[:, LNG:LNG + D], ALU.mult, r=[sq.r, V.r], w=[sq.r])
        tt(vn.t[:], sq.t[:], V.t[:, LNB:LNB + D], ALU.add, r=[sq.r, V.r], w=[vn.r])
        pump(3)
        for g in range(8):
            mm(pY[g // 4].t[:, (g % 4) * 128:(g % 4 + 1) * 128], WmT[l].t[:, g, :], vn.t[:, g * 128:(g + 1) * 128], True, True,
               r=[WmT[l].r, vn.r], w=[pY[g // 4].r])
        for hf in range(2):
            cs = slice(hf * 512, (hf + 1) * 512)
            tt(sq.t[:, cs].rearrange("p (g d) -> p g d", d=128), pY[hf].t[:].rearrange("p (g d) -> p g d", d=128),
               sgb[l].t[:, 4 * hf:4 * hf + 4].unsqueeze(2).broadcast_to([128, 4, 128]), ALU.add, r=[pY[hf].r, sgb[l].r], w=[sq.r])
            tt(sq.t[:, cs], sq.t[:, cs], usb.t[:, cs], ALU.mult, r=[sq.r, usb.r], w=[sq.r])
            tt(ytm.t[:, cs], sq.t[:, cs], zs.t[:, cs], ALU.mult, r=[sq.r, zs.r], w=[ytm.r])
            pump(2)
        transposes(lambda j: ytm.t[:, j * 128:(j + 1) * 128], 8, lambda r0, m: yT.t[:, 8 + r0:8 + r0 + m, :], r=[ytm.r], w=[yT.r])

        pump()
        transposes(lambda j: xcT.t[:, j, :], 16, lambda r0, m: xtm.t[:, r0 * 128:(r0 + m) * 128].rearrange("p (j d) -> p j d", d=128),
                   r=[xcT.r], w=[xtm.r])
        transposes(lambda j: xcT.t[:, 16 + j, :], 4, lambda r0, m: Btm.t[:, 0:4, :], r=[xcT.r], w=[Btm.r])
        mm(pV.t[:, 0:32], U, sm.t[:, 1, :], True, True, r=[cst.r, sm.r], w=[pV.r])
        mm(pV.t[:, 32:64], ones, sm.t[:, 1, :], True, True, r=[cst.r, sm.r], w=[pV.r])
        acopy(sm.t[:, 2, :], pV.t[:, 0:32], r=[pV.r], w=[sm.r])
        acopy(sm.t[:, 4, :], pV.t[:, 32:64], r=[pV.r], w=[sm.r])
        act(sm.t[:, 3, :], sm.t[:, 2, :], AF.Exp, r=[sm.r], w=[sm.r])
        tt(sm.t[:, 5, :], sm.t[:, 4, :], sm.t[:, 2, :], ALU.subtract, r=[sm.r], w=[sm.r])
        act(sm.t[:, 5, :], sm.t[:, 5, :], AF.Exp, r=[sm.r], w=[sm.r])
        act(sm.t[:, 6, :], sm.t[:, 4, :], AF.Exp, r=[sm.r], w=[sm.r])
        v3 = lambda t_, : t_.t[:].rearrange("p (h d) -> p h d", d=64)
        bc = lambda ap: ap.unsqueeze(2).broadcast_to([128, 32, 64])
        tt(v3(xdt), v3(xtm), bc(sm.t[:, 0, :]), ALU.mult, r=[xtm.r, sm.r], w=[xdt.r])
        tt(v3(xD), v3(xtm), bc(vsm[l].t[:, 80:112]), ALU.mult, r=[xtm.r, vsm[l].r], w=[xD.r])
        tt(v3(xw), v3(xdt), bc(sm.t[:, 5, :]), ALU.mult, r=[xdt.r, sm.r], w=[xw.r])
        for g in range(4):
            mm(pS[0].t[:, g, :], xcT.t[:, 16 + g, :], xcT.t[:, 20 + g, :], True, True, r=[xcT.r], w=[pS[0].r])
        tt(CBm.t[:], pS[0].t[:], U.unsqueeze(1).broadcast_to([128, 4, 128]), ALU.mult, r=[pS[0].r, cst.r], w=[CBm.r])
        for i in range(6):
            pending.append(lambda i=i: proj_tm(l, 23 + i, lambda a, n, i=i: act(gate.t[:, i * 512:(i + 1) * 512], a.t[:, :], AF.Sigmoid, r=[a.r], w=[gate.r])))
        for g in range(4):
            gs = slice(g * 512, (g + 1) * 512)
            tt(AL.t[:], Lm.unsqueeze(1).broadcast_to([128, 8, 128]), sm.t[:, 1, 8 * g:8 * g + 8].unsqueeze(2).broadcast_to([128, 8, 128]), ALU.mult,
               r=[cst.r, sm.r], w=[AL.r])
            mm(pY[0].t[:], identb.t[:], xD.t[:, gs], True, False, r=[identb.r, xD.r], w=[pY[0].r])
            if c > 0:
                mm(pY[1].t[:], xcT.t[:, 20 + g, :], Sb[l].t[:, gs], True, True, r=[xcT.r, Sb[l].r], w=[pY[1].r])
            for hf in range(2):
                for hh in range(4):
                    h = 8 * g + 4 * hf + hh
                    mm(pS[1].t[:, hh, :], AL.t[:, 4 * hf + hh, :], U, True, True, r=[AL.r, cst.r], w=[pS[1].r])
                act(dec4.t[:], pS[1].t[:], AF.Exp, r=[pS[1].r], w=[dec4.r])
                tt(MT.t[:], dec4.t[:], CBm.t[:, g, :].unsqueeze(1).broadcast_to([128, 4, 128]), ALU.mult, r=[dec4.r, CBm.r], w=[MT.r])
                for hh in range(4):
                    h = 8 * g + 4 * hf + hh
                    mm(pY[0].t[:, (4 * hf + hh) * 64:(4 * hf + hh + 1) * 64], MT.t[:, hh, :], xdt.t[:, h * 64:(h + 1) * 64],
                       False, (hf == 1 and hh == 3), r=[MT.r, xdt.r], w=[pY[0].r])
            if c > 0:
                tt(ytmp.t[:].rearrange("p (h d) -> p h d", d=64), pY[1].t[:].rearrange("p (h d) -> p h d", d=64),
                   sm.t[:, 3, 8 * g:8 * g + 8].unsqueeze(2).broadcast_to([128, 8, 64]), ALU.mult, r=[pY[1].r, sm.r], w=[ytmp.r])
                tt(yssm.t[:, gs], pY[0].t[:], ytmp.t[:], ALU.add, r=[pY[0].r, ytmp.r], w=[yssm.r])
            else:
                acopy(yssm.t[:, gs], pY[0].t[:], r=[pY[0].r], w=[yssm.r])
            pump(1)
        for g in range(4):
            gs = slice(g * 512, (g + 1) * 512)
            a = next_acc()
            mm(a.t[:], Btm.t[:, g, :], xw.t[:, gs], True, True, r=[Btm.r, xw.r], w=[a.r])
            if c > 0:
                tt(St[l].t[:, gs].rearrange("p (h d) -> p h d", d=64), St[l].t[:, gs].rearrange("p (h d) -> p h d", d=64),
                   sm.t[:, 6, 8 * g:8 * g + 8].unsqueeze(2).broadcast_to([128, 8, 64]), ALU.mult, r=[St[l].r, sm.r], w=[St[l].r])
                tt(St[l].t[:, gs], St[l].t[:, gs], a.t[:], ALU.add, r=[St[l].r, a.r], w=[St[l].r])
            else:
                acopy(St[l].t[:, gs], a.t[:], r=[a.r], w=[St[l].r])
        acopy(Sb[l].t[:], St[l].t[:], r=[St[l].r], w=[Sb[l].r])
        tt(yssm.t[:], yssm.t[:], zm.t[:], ALU.mult, r=[yssm.r, zm.r], w=[yssm.r])
        for g in range(4):
            act(sq.t[:, 0:512], yssm.t[:, g * 512:(g + 1) * 512], AF.Square, r=[yssm.r], w=[sq.r, ss.r], accum_out=ss.t[:, 8 + g:9 + g])
        ts(ss.t[:, 8:12], ss.t[:, 8:12], 1.0 / 512, EPS, ALU.mult, ALU.add, r=[ss.r], w=[ss.r])
        act(ss.t[:, 8:12], ss.t[:, 8:12], AF.Sqrt, r=[ss.r], w=[ss.r])
        I("dve", lambda e: e.reciprocal(out=ss.t[:, 8:12], in_=ss.t[:, 8:12]), r=[ss.r], w=[ss.r])
        tt(yssm.t[:].rearrange("p (g d) -> p g d", d=512), yssm.t[:].rearrange("p (g d) -> p g d", d=512),
           ss.t[:, 8:12].unsqueeze(2).broadcast_to([128, 4, 512]), ALU.mult, r=[yssm.r, ss.r], w=[yssm.r])
        tt(ytm.t[:], yssm.t[:], V.t[:, NG:NG + 2048], ALU.mult, r=[yssm.r, V.r], w=[ytm.r])
        transposes(lambda j: ytm.t[:, j * 128:(j + 1) * 128], 16, lambda r0, m: yT.t[:, 16 + r0:16 + r0 + m, :], r=[ytm.r], w=[yT.r])

        pump()

        for br, (t0, kb0, nk) in enumerate([(29, 0, 1), (31, 8, 1), (33, 16, 2)]):
            for ct in range(2):
                cs = slice(ct * 512, (ct + 1) * 512)
                slots = [load_w(l, t0 + ct + 2 * kh) for kh in range(nk)]
                a = next_acc()
                for kh in range(nk):
                    for k in range(8):
                        mm(a.t[:], yT.t[:, kb0 + kh * 8 + k, :], wt[slots[kh]].t[:, k, :], (kh == 0 and k == 0), (kh == nk - 1 and k == 7),
                           r=[yT.r, wt[slots[kh]].r], w=[a.r])
                gsl = gate.t[:, br * 1024 + ct * 512: br * 1024 + (ct + 1) * 512]
                if br == 0:
                    tt(merged.t[:, cs], a.t[:], gsl, ALU.mult, r=[a.r, gate.r], w=[merged.r])
                else:
                    tt(mtmp.t[:], a.t[:], gsl, ALU.mult, r=[a.r, gate.r], w=[mtmp.r])
                    tt(merged.t[:, cs], merged.t[:, cs], mtmp.t[:], ALU.add, r=[merged.r, mtmp.r], w=[merged.r])
        acopy(mb.t[:], merged.t[:], r=[merged.r], w=[mb.r])
        transposes(lambda j: mb.t[:, j * 128:(j + 1) * 128], 8, lambda r0, m: mT.t[:, r0:r0 + m, :], r=[mb.r], w=[mT.r])
        for ct in range(2):
            slot = load_w(l, 37 + ct)
            a = next_acc()
            for k in range(8):
                mm(a.t[:], mT.t[:, k, :], wt[slot].t[:, k, :], k == 0, k == 7, r=[mT.r, wt[slot].r], w=[a.r])
            acopy(osb.t[:, ct * 512:(ct + 1) * 512], a.t[:], r=[a.r], w=[osb.r])
        act(sq.t[:], osb.t[:], AF.Square, r=[osb.r], w=[sq.r, ss.r], accum_out=ss.t[:, 12:13])
        rstd_from(12, 1.0 / D)
        stt(osb.t[:], osb.t[:], ss.t[:, 12:13], V.t[:, GPOST:GPOST + D], ALU.mult, ALU.mult, r=[osb.r, ss.r, V.r], w=[osb.r])
        tt(xout.t[:], xin.t[:], osb.t[:], ALU.add, r=[xin.r, osb.r], w=[xout.r])

    def load_x(c):
        b = xa[c % 2]
        P.dma("sp", lambda e: e.dma_start(out=b.t[:], in_=x_d[c * 128:(c + 1) * 128, :]), w=[b.r])

    load_x(0)
    for c in range(NCH):
        if c + 1 < NCH:
            load_x(c + 1)
        if depth == 2:
            chunk_layer(0, c, xa[c % 2], x1)
            chunk_layer(1, c, x1, xo)
        else:
            chunk_layer(0, c, xa[c % 2], xo)
        P.dma("sp", lambda e, c=c: e.dma_start(out=y_d[c * 128:(c + 1) * 128, :], in_=xo.t[:]), r=[xo.r], w=[Ry], dsem=ysem)
    P.final_wait("sp", ysem)
    P.emit()
    es.close()
    return nc, P


def _t5_bucket(dist):
    max_exact = 16
    dist_f = np.maximum(dist, 1).astype(np.float32)
    large = max_exact + (np.log(dist_f / np.float32(max_exact)) / np.float32(np.log(128 / max_exact))
                         * np.float32(32 - max_exact)).astype(np.int32)
    large = np.minimum(large, 31)
    return np.where(dist < max_exact, dist, large)


def host_prep(inputs, S):
    f = lambda a: np.ascontiguousarray(np.asarray(a, dtype=np.float32))
    w_in = f(inputs["w_in"])
    kcols = w_in[:, :, 1024:1152]
    w_kd = np.concatenate([kcols[:, :, 0:64], kcols[:, :, 0:64], kcols[:, :, 64:128], kcols[:, :, 64:128]], axis=2)
    vec = np.concatenate([f(inputs["norm_pre"]), f(inputs["norm_post"]), f(inputs["sg_ln_g"]), f(inputs["sg_ln_b"]),
                          f(inputs["ssm_norm_g"]), f(inputs["att_sinks"]), f(inputs["ssm_dt_bias"]), f(inputs["ssm_a_log"]),
                          f(inputs["ssm_d"])], axis=1)
    vecs = np.ascontiguousarray(np.broadcast_to(vec[:, None, :], (2, 128, NV)))
    cwf = np.concatenate([f(inputs["ssm_conv_w"]), f(inputs["ssm_conv_b"])[:, None, :]], axis=1)
    convw = np.ascontiguousarray(cwf.reshape(2, 5, 24, 128).transpose(0, 3, 2, 1))
    sgb = np.ascontiguousarray(f(inputs["sg_b"]).transpose(0, 2, 1))
    sgwT = np.ascontiguousarray(f(inputs["sg_w"]).transpose(0, 3, 1, 2))
    rel = f(inputs["rel_bias"])
    table = np.concatenate([rel, np.full((1, 16), NEG, np.float32)], axis=0)
    s = np.arange(128)[:, None]; q = np.arange(128)[None, :]
    d0 = q + 128 - s; valid0 = (d0 >= 0) & (d0 < 128)
    d1 = q - s; valid1 = d1 >= 0
    idx0 = np.where(valid0, _t5_bucket(np.maximum(d0, 0)), 32)
    idx1 = np.where(valid1, _t5_bucket(np.maximum(d1, 0)), 32)
    idx = np.stack([idx0, idx1], axis=1)
    bt = table[idx].transpose(0, 1, 3, 2)
    bt = bt.reshape(128, 2, 4, 2, 2, 128)
    biasT = np.ascontiguousarray(bt.transpose(0, 2, 4, 1, 3, 5).reshape(128, 4, 2, 4, 128))
    k = np.arange(128)[:, None]; j = np.arange(128)[None, :]
    consts = np.concatenate([(k == j), (k <= j), (k > j), np.ones((128, 128), bool)], axis=1).astype(np.float32)
    common = {"w_in": w_in, "w_kd": np.ascontiguousarray(w_kd), "w_ba": f(inputs["w_br_att"]), "w_bs": f(inputs["w_br_sg"]),
              "w_bm": f(inputs["w_br_ssm"]), "w_o": f(inputs["w_out"]), "vecs": vecs, "convw": convw, "sgb": sgb,
              "sgwT": sgwT, "biasT": biasT, "consts": np.ascontiguousarray(consts)}
    return common


_CACHE = {}


def kernel(**inputs):
    x = np.asarray(inputs["x"], dtype=np.float32)
    B, S, _ = x.shape
    if S not in _CACHE:
        _CACHE[S] = build(S)[0]
    nc = _CACHE[S]
    common = host_prep(inputs, S)
    in_maps = []
    for core in range(B):
        m = dict(common)
        m["x"] = np.ascontiguousarray(x[core])
        in_maps.append(m)
    res = run_bass_kernel_spmd(nc, in_maps, core_ids=list(range(B)))
    out = np.stack([res.results[b]["y"] for b in range(B)], axis=0)
    return out.astype(np.float32)
```
